# Optimizing a Trainium2 kernel written in Bass

```python
import jax, jax.numpy as jnp
from jax import lax
import numpy as np

D_MODEL = 1024
BATCH = 16
SEQ = 256
DEPTH = 2
DEC_BATCH = 2
DEC_SEQ = 1024
PAST_LEN = 256

GRID_W = 64
N_RET_HEADS = 4
RET_DK = 128
RET_DV = 256
RET_CHUNK = 128
N_GMLP_GROUPS = 4
GMLP_CH = 128
GMLP_CHUNK = 128
RET_QK_W = N_RET_HEADS * RET_DK
RET_V_W = N_RET_HEADS * RET_DV
GMLP_W = N_GMLP_GROUPS * GMLP_CH
EVEN_IN_W = 2 * RET_QK_W + 2 * RET_V_W + 2 * GMLP_W
EVEN_OUT_W = RET_V_W + GMLP_W
CONV_CH = D_MODEL
CONV_K = 3
D_FF = 4 * D_MODEL
N_MOD = 6
EPS = 1e-6

kernel_name = "hybrid_retention_gmlp_shortconv_diffusion_step"


def rms_norm(x, g):
    xf = x.astype(jnp.float32)
    y = xf * lax.rsqrt(jnp.mean(xf * xf, axis=-1, keepdims=True) + EPS)
    return (y * g.astype(jnp.float32)).astype(x.dtype)


def adaln(cond, w_mod, b_mod):
    m = (jax.nn.silu(cond) @ w_mod + b_mod)[:, None, :]
    return jnp.split(m, N_MOD, axis=-1)


def _retention_dir(q, k, v, log_gamma, s0):
    b, L, h, _ = q.shape
    dv = v.shape[-1]
    n = L // RET_CHUNK

    def to_chunks(t):
        return t.reshape(b, n, RET_CHUNK, h, t.shape[-1]).transpose(1, 0, 3, 2, 4)

    qc, kc, vc = to_chunks(q), to_chunks(k), to_chunks(v)
    pos = jnp.arange(RET_CHUNK, dtype=jnp.float32)
    diff = pos[:, None] - pos[None, :]
    decay_mask = jnp.where(diff >= 0, jnp.exp(jnp.maximum(diff, 0.0) * log_gamma[:, None, None]), 0.0)
    q_decay = jnp.exp((pos + 1.0) * log_gamma[:, None])[:, :, None]
    k_decay = jnp.exp((RET_CHUNK - 1.0 - pos) * log_gamma[:, None])[:, :, None]
    chunk_decay = jnp.exp(RET_CHUNK * log_gamma)[:, None, None]

    def step(s, inp):
        qi, ki, vi = inp
        scores = jnp.einsum("bhid,bhjd->bhij", qi, ki) * decay_mask
        inner = jnp.einsum("bhij,bhje->bhie", scores, vi)
        cross = jnp.einsum("bhid,bhde->bhie", qi, s) * q_decay
        s_new = chunk_decay * s + jnp.einsum("bhjd,bhje->bhde", ki * k_decay, vi)
        return s_new, inner + cross

    s_fin, out = lax.scan(step, s0, (qc, kc, vc))
    out = out.transpose(1, 0, 3, 2, 4).reshape(b, L, h, dv)
    return out, s_fin


def bidir_retention(q, k, v, log_gamma, s0):
    o_f, s_f = _retention_dir(q, k, v, log_gamma[0], s0[:, 0])
    o_b, s_b = _retention_dir(jnp.flip(q, 1), jnp.flip(k, 1), jnp.flip(v, 1), log_gamma[1], s0[:, 1])
    return o_f + jnp.flip(o_b, 1), jnp.stack([s_f, s_b], axis=1)


def even_mixer(h, s0, w_in, ret_decay_exp, gmlp_ws, gmlp_b, w_out):
    b, L, _ = h.shape
    proj = h @ w_in
    q, k, v, g, u, z = jnp.split(
        proj,
        [RET_QK_W, 2 * RET_QK_W, 2 * RET_QK_W + RET_V_W, 2 * RET_QK_W + 2 * RET_V_W,
         2 * RET_QK_W + 2 * RET_V_W + GMLP_W],
        axis=-1)
    qf = q.reshape(b, L, N_RET_HEADS, RET_DK).astype(jnp.float32)
    kf = k.reshape(b, L, N_RET_HEADS, RET_DK).astype(jnp.float32) * (RET_DK ** -0.5)
    vf = v.reshape(b, L, N_RET_HEADS, RET_DV).astype(jnp.float32)
    log_gamma = jnp.log1p(-jnp.exp2(-ret_decay_exp.astype(jnp.float32)))
    o, s_fin = bidir_retention(qf, kf, vf, log_gamma, s0.astype(jnp.float32))
    o = o * lax.rsqrt(jnp.mean(o * o, axis=-1, keepdims=True) + EPS)
    ret_out = (jax.nn.silu(g.astype(jnp.float32)) * o.reshape(b, L, RET_V_W)).astype(h.dtype)
    u = jax.nn.gelu(u)
    zf = jax.nn.gelu(z).astype(jnp.float32)
    mu = jnp.mean(zf, axis=-1, keepdims=True)
    zf = (zf - mu) * lax.rsqrt(jnp.mean((zf - mu) ** 2, axis=-1, keepdims=True) + EPS)
    zc = zf.astype(h.dtype).reshape(b, L // GMLP_CHUNK, GMLP_CHUNK, N_GMLP_GROUPS, GMLP_CH)
    sv = jnp.einsum("gpq,bnqgc->bnpgc", gmlp_ws, zc) + gmlp_b.T[None, None, :, :, None]
    gm_out = u * sv.reshape(b, L, GMLP_W)
    out = jnp.concatenate([ret_out, gm_out], axis=-1) @ w_out
    return out, s_fin.astype(h.dtype)


def conv3(x, w):
    pad = [(0, 0)] * (x.ndim - 2) + [(1, 1), (0, 0)]
    xp = jnp.pad(x, pad)
    return w[0] * xp[..., :-2, :] + w[1] * xp[..., 1:-1, :] + w[2] * xp[..., 2:, :]


def odd_mixer(h, w_in, conv_w, w_out, rows):
    b, L, _ = h.shape
    bg, cg, hv = jnp.split(h @ w_in, 3, axis=-1)
    xc = cg * hv
    if rows is None:
        yc = conv3(xc, conv_w)
    else:
        yc = conv3(xc.reshape(b, rows, GRID_W, CONV_CH), conv_w).reshape(b, L, CONV_CH)
    return (bg * yc) @ w_out


def channel_mlp(h, w1, w2):
    a = jax.nn.relu(h @ w1)
    return (a * a) @ w2


def trunk(x, cond, ret_states, layers, final_norm, rows):
    new_states = []
    for i in range(DEPTH):
        p = layers[i]
        sh1, sc1, g1, sh2, sc2, g2 = adaln(cond, p["w_mod"], p["b_mod"])
        hmod = rms_norm(x, p["norm1"]) * (1.0 + sc1) + sh1
        if i % 2 == 0:
            mix, s_new = even_mixer(hmod, ret_states[i // 2], p["w_in"], p["ret_decay_exp"],
                                    p["gmlp_ws"], p["gmlp_b"], p["w_out"])
            new_states.append(s_new)
        else:
            mix = odd_mixer(hmod, p["w_in"], p["conv_w"], p["w_out"], rows)
        x = x + g1 * mix
        hmod = rms_norm(x, p["norm2"]) * (1.0 + sc2) + sh2
        x = x + g2 * channel_mlp(hmod, p["ffn_w1"], p["ffn_w2"])
    return rms_norm(x, final_norm), new_states


def setup_inputs(seed: int = 0) -> dict:
    key = jax.random.key(seed)
    ks = jax.random.split(key, 32)
    nrm = lambda k, shape, s: jax.random.normal(k, shape, jnp.float32) * s
    d = D_MODEL
    inp = {}
    inp["x_prompt"] = nrm(ks[0], (BATCH, SEQ, d), 1.0)
    inp["x_sample"] = nrm(ks[1], (DEC_BATCH, DEC_SEQ, d), 1.0)
    inp["state_l0_ret"] = nrm(ks[2], (DEC_BATCH, 2, N_RET_HEADS, RET_DK, RET_DV), 0.5)
    inp["c"] = nrm(ks[3], (DEC_BATCH, d), 1.0)
    inp["c_ctx"] = nrm(ks[4], (d,), 1.0)
    inp["l0_norm1"] = 1.0 + nrm(ks[5], (d,), 0.1)
    inp["l0_w_in"] = nrm(ks[6], (d, EVEN_IN_W), d ** -0.5)
    inp["l0_ret_decay_exp"] = (5.0 + jnp.arange(N_RET_HEADS, dtype=jnp.float32))[None, :] + nrm(ks[7], (2, N_RET_HEADS), 0.1)
    inp["l0_gmlp_ws"] = nrm(ks[8], (N_GMLP_GROUPS, GMLP_CHUNK, GMLP_CHUNK), GMLP_CHUNK ** -0.5)
    inp["l0_gmlp_b"] = 1.0 + nrm(ks[9], (N_GMLP_GROUPS, GMLP_CHUNK), 0.1)
    inp["l0_w_out"] = nrm(ks[10], (EVEN_OUT_W, d), EVEN_OUT_W ** -0.5)
    inp["l0_norm2"] = 1.0 + nrm(ks[11], (d,), 0.1)
    inp["l0_w_mod"] = nrm(ks[12], (d, N_MOD * d), 0.5 * d ** -0.5)
    inp["l0_b_mod"] = nrm(ks[13], (N_MOD * d,), 0.02)
    inp["l0_ffn_w1"] = nrm(ks[14], (d, D_FF), d ** -0.5)
    inp["l0_ffn_w2"] = nrm(ks[15], (D_FF, d), D_FF ** -0.5)
    inp["l1_norm1"] = 1.0 + nrm(ks[16], (d,), 0.1)
    inp["l1_w_in"] = nrm(ks[17], (d, 3 * CONV_CH), d ** -0.5)
    inp["l1_conv_w"] = nrm(ks[18], (CONV_K, CONV_CH), CONV_K ** -0.5)
    inp["l1_w_out"] = nrm(ks[19], (CONV_CH, d), CONV_CH ** -0.5)
    inp["l1_norm2"] = 1.0 + nrm(ks[20], (d,), 0.1)
    inp["l1_w_mod"] = nrm(ks[21], (d, N_MOD * d), 0.5 * d ** -0.5)
    inp["l1_b_mod"] = nrm(ks[22], (N_MOD * d,), 0.02)
    inp["l1_ffn_w1"] = nrm(ks[23], (d, D_FF), d ** -0.5)
    inp["l1_ffn_w2"] = nrm(ks[24], (D_FF, d), D_FF ** -0.5)
    inp["final_norm"] = 1.0 + nrm(ks[25], (d,), 0.1)
    return inp


def reference(x_prompt, x_sample, state_l0_ret, c, c_ctx,
              l0_norm1, l0_w_in, l0_ret_decay_exp, l0_gmlp_ws, l0_gmlp_b, l0_w_out, l0_norm2,
              l0_w_mod, l0_b_mod, l0_ffn_w1, l0_ffn_w2,
              l1_norm1, l1_w_in, l1_conv_w, l1_w_out, l1_norm2, l1_w_mod, l1_b_mod, l1_ffn_w1, l1_ffn_w2,
              final_norm):
    layers = [
        {"norm1": l0_norm1, "w_in": l0_w_in, "ret_decay_exp": l0_ret_decay_exp, "gmlp_ws": l0_gmlp_ws,
         "gmlp_b": l0_gmlp_b, "w_out": l0_w_out, "norm2": l0_norm2, "w_mod": l0_w_mod, "b_mod": l0_b_mod,
         "ffn_w1": l0_ffn_w1, "ffn_w2": l0_ffn_w2},
        {"norm1": l1_norm1, "w_in": l1_w_in, "conv_w": l1_conv_w, "w_out": l1_w_out, "norm2": l1_norm2,
         "w_mod": l1_w_mod, "b_mod": l1_b_mod, "ffn_w1": l1_ffn_w1, "ffn_w2": l1_ffn_w2},
    ]
    zero_state = jnp.zeros((x_prompt.shape[0], 2, N_RET_HEADS, RET_DK, RET_DV), x_prompt.dtype)
    y_prompt, ctx_states = trunk(x_prompt, c_ctx[None, :], [zero_state], layers, final_norm, None)
    new_state_l0_ret = ctx_states[0]
    rows = x_sample.shape[1] // GRID_W
    y_sample, _ = trunk(x_sample, c, [state_l0_ret], layers, final_norm, rows)
    return (y_prompt, y_sample, new_state_l0_ret)
```

```python
import contextlib
import numpy as np
import concourse.bass as bass
import concourse.mybir as mybir
from concourse.bass_utils import run_bass_kernel_spmd

F32 = mybir.dt.float32
BF16 = mybir.dt.bfloat16
AF = mybir.ActivationFunctionType
OP = mybir.AluOpType

D = 1024
NT = 768
EPS = 1e-6
NSLOT = 6
INFLIGHT = 3
SLOT_ELEMS = 4096
GELU = AF.Gelu_apprx_tanh


class Prog:
    def __init__(self, nc):
        self.nc = nc
        self.eng = {"pe": nc.tensor, "act": nc.scalar, "dve": nc.vector, "pool": nc.gpsimd, "sp": nc.sync}
        self.sems = {}
        self.cnt = {}
        self.known = {e: {} for e in self.eng}
        self.lw = {}
        self.rd = {}
        self.epoch = 0
        self.frontier = {}
        self.key_epoch = {}
        self.clock = {}
        self.nwait = 0
        self.nskip = 0
        for e in ("pe", "act", "dve", "pool"):
            self.newsem("c_" + e)

    def newsem(self, name):
        self.sems[name] = self.nc.alloc_semaphore(name)
        self.cnt[name] = 0
        return name

    def op(self, eng, emit, reads=(), writes=(), dsem=None):
        own = "c_" + eng
        waits = {}

        def need(sv, raw, war=False):
            s, v = sv
            if s == own and dsem is None:
                if eng == "pe" or war:
                    return
            waits[s] = max(waits.get(s, 0), v)

        for k in reads:
            if k in self.lw:
                need(self.lw[k], True)
            if k.startswith("ps"):
                for sv in self.rd.get(k, ()):
                    if sv[0] != own:
                        need(sv, False)
        for k in writes:
            if k in self.lw:
                need(self.lw[k], False)
            for sv in self.rd.get(k, ()):
                need(sv, False, war=True)
            if self.frontier and not k.startswith(self.PERSISTENT) and self.key_epoch.get(k, -1) < self.epoch:
                for sv in self.frontier.items():
                    need(sv, False)
            self.key_epoch[k] = self.epoch
        e = self.eng[eng]
        kn = self.known[eng]
        for s, v in sorted(waits.items(), key=lambda sv: -sv[1]):
            if kn.get(s, 0) >= v:
                self.nskip += 1
                continue
            kn[s] = v
            e.wait_ge(self.sems[s], v)
            self.nwait += 1
            for s2, v2 in self.clock.get((s, v), {}).items():
                if kn.get(s2, 0) < v2:
                    kn[s2] = v2
        inst = emit(e)
        if dsem is not None:
            s, inc = dsem, 16
        else:
            s, inc = own, 1
        self.cnt[s] += inc
        val = self.cnt[s]
        inst.then_inc(self.sems[s], inc)
        self.clock[(s, val)] = dict(kn)
        for k in writes:
            self.lw[k] = (s, val)
            self.rd[k] = []
        for k in reads:
            self.rd.setdefault(k, []).append((s, val))

    PERSISTENT = ("ps", "xT", "hT", "ntmp", "nsq", "mod", "Amod", "slot", "st6", "mst", "dmaorder", "ident", "ones",
                  "vecAT", "vecBT", "scT", "lgB", "cdB", "c0B", "colsT", "Mmask", "qdec", "KD", "kd4", "kdo", "wsT")

    def soft_barrier(self):
        self.epoch += 1
        fr = dict(self.frontier)
        for s_, v in self.cnt.items():
            if s_.startswith("ring") or v == 0:
                continue
            fr[s_] = max(fr.get(s_, 0), v)
        self.frontier = fr

    def barrier(self, engines=("pe", "act", "dve", "sp"), skip_prefix="ring"):
        for eng in engines:
            e = self.eng[eng]
            for s, v in self.cnt.items():
                if s.startswith(skip_prefix) or s == "c_pool" or v == 0:
                    continue
                if s == "c_" + eng:
                    continue
                if self.known[eng].get(s, 0) >= v:
                    continue
                self.known[eng][s] = v
                e.wait_ge(self.sems[s], v)

    def final_wait(self, eng="sp"):
        e = self.eng[eng]
        for s, v in self.cnt.items():
            if v == 0:
                continue
            e.wait_ge(self.sems[s], v)


class _Stop(Exception):
    pass


def build_program():
    nc = bass.Bass("TRN2", target_bir_lowering=False)
    p = Prog(nc)
    try:
        _build(nc, p)
    except _Stop:
        p.final_wait("sp")
    return nc


def _build(nc, p):
    import os
    stop_at = os.environ.get("STOP_AT", "")

    def stage(tag):
        if stop_at == tag:
            raise _Stop()

    def din(name, shape):
        return nc.dram_tensor(name, list(shape), F32, kind="ExternalInput").ap()

    xin = din("xin", [NT, D])
    xoth = din("xoth", [NT, D])
    s0 = din("s0", [2, 4, 128, 256])
    vecA = din("vecA", [96, 128])
    vecB = din("vecB", [84, 128])
    cst = din("cst", [128, 6, 128])
    cols = din("cols", [128, 16])
    rde = din("rde", [128, 8])
    ident_d = din("ident_d", [128, 128])
    gws = din("gws", [4, 128, 128])
    W = {}
    for l in (0, 1):
        W[(l, "mod")] = din(f"l{l}_w_mod", [D, 6 * D])
        W[(l, "w1")] = din(f"l{l}_ffn_w1", [D, 4 * D])
        W[(l, "w2")] = din(f"l{l}_ffn_w2", [4 * D, D])
    W[(0, "in")] = din("l0_w_in", [D, 4096])
    W[(0, "out")] = din("l0_w_out", [1536, D])
    W[(1, "in")] = din("l1_w_in", [D, 3072])
    W[(1, "out")] = din("l1_w_out", [D, D])
    y_out = nc.dram_tensor("y", [NT, D], F32, kind="ExternalOutput").ap()
    ns_out = nc.dram_tensor("ns", [2, 2, 4, 128, 256], F32, kind="ExternalOutput").ap()

    top = contextlib.ExitStack()

    uniq = {"i": 0}

    def sb(es, name, shape, dt=F32):
        uniq["i"] += 1
        return es.enter_context(nc.sbuf_tensor(f"{name}_u{uniq['i']}", list(shape), dt))

    xT = sb(top, "xT", [128, 8, NT])
    hT = sb(top, "hT", [128, 8, NT], BF16)
    ring = [sb(top, f"ring{i}", [128, SLOT_ELEMS], BF16) for i in range(NSLOT)]
    ident = sb(top, "ident", [128, 128])
    ones_bf = sb(top, "ones_bf", [128, 128], BF16)
    ones_f = sb(top, "ones_f", [128, 128])
    vecAT = sb(top, "vecAT", [128, 96])
    vecBT = sb(top, "vecBT", [128, 84])
    scT = sb(top, "scT", [128, 8, 2], BF16)
    mod = [sb(top, f"mod{l}", [128, 48, 2]) for l in (0, 1)]
    Amod = sb(top, "Amod", [128, 4, 8, 2])
    lgB = sb(top, "lgB", [128, 8])
    cdB = sb(top, "cdB", [128, 8])
    c0B = sb(top, "c0B", [128, 8])
    colsT = sb(top, "colsT", [128, 16])
    Mmask = sb(top, "Mmask", [128, 4, 128])
    qdec = sb(top, "qdec", [128, 2, 4, 128])
    KD = sb(top, "KD", [128, 2, 512])
    kd4 = sb(top, "kd4", [128, 8])
    kdo = sb(top, "kdo", [128, 6, 8])
    wsT = sb(top, "wsT", [128, 4, 128], BF16)
    small = sb(top, "small", [128, 32])
    mst = sb(top, "mst", [2, 512])
    nsq = sb(top, "nsq", [128, 2, NT], BF16)
    ntmp = sb(top, "ntmp", [128, 2, NT])
    st6 = sb(top, "st6", [128, 16])
    PS = [top.enter_context(nc.psum_tensor(f"ps{i}", [128, 512], F32)) for i in range(8)]
    PK = [f"ps{i}" for i in range(8)]

    for i in range(NSLOT):
        p.newsem(f"ring{i}")
    for n in ("ld_a", "ld_b", "ld_c", "ld_d", "ld_e", "ld_f", "ld_g", "ld_h", "xs0", "xs1", "st_y0", "st_y1", "st_ns"):
        p.newsem(n)

    blocks = []

    def add_blocks(w, K, c0, c1, cw):
        wv = w.rearrange("(k p) n -> p k n", p=128)
        nk = K // 128
        for c in range(c0, c1, cw):
            ns = (nk * cw + SLOT_ELEMS - 1) // SLOT_ELEMS
            blocks.append((wv[:, :, c:c + cw], nk, cw, ns))

    add_blocks(W[(0, "mod")], D, 0, 2048, 512)
    add_blocks(W[(0, "in")], D, 0, 2048, 512)
    add_blocks(W[(0, "in")], D, 3584, 4096, 512)
    add_blocks(W[(0, "in")], D, 3072, 3584, 512)
    add_blocks(W[(0, "in")], D, 2048, 3072, 512)
    add_blocks(W[(0, "mod")], D, 2048, 3072, 512)
    for i_ in range(4):
        add_blocks(W[(0, "out")], 1536, 256 * i_, 256 * i_ + 256, 256)
        add_blocks(W[(0, "mod")], D, 3072 + 512 * i_, 3072 + 512 * i_ + 512, 512)
    add_blocks(W[(0, "mod")], D, 5120, 6144, 512)
    add_blocks(W[(1, "mod")], D, 0, 2048, 512)
    add_blocks(W[(0, "w1")], D, 0, 4096, 512)
    add_blocks(W[(0, "w2")], 4096, 0, 1024, 256)
    add_blocks(W[(1, "mod")], D, 2048, 3072, 512)
    add_blocks(W[(1, "in")], D, 0, 3072, 512)
    add_blocks(W[(1, "out")], D, 0, 512, 512)
    add_blocks(W[(1, "mod")], D, 3072, 3584, 512)
    add_blocks(W[(1, "out")], D, 512, 1024, 512)
    add_blocks(W[(1, "mod")], D, 3584, 5120, 512)
    add_blocks(W[(1, "mod")], D, 5120, 6144, 512)
    add_blocks(W[(1, "w1")], D, 0, 4096, 512)
    add_blocks(W[(1, "w2")], 4096, 0, 1024, 256)

    rstate = {"next_load": 0, "next_use": 0, "next_get": 0, "pos": 0, "loaded": []}

    def ring_issue_one():
        i = rstate["next_load"]
        if i >= len(blocks):
            return False
        src, nk, cw, ns = blocks[i]
        pos = rstate["pos"]
        if pos + ns > NSLOT:
            pos = 0
        occ = set()
        for (j, pj, nsj) in rstate["loaded"]:
            if j >= rstate["next_use"]:
                occ.update(range(pj, pj + nsj))
        if any(s in occ for s in range(pos, pos + ns)):
            return False
        rstate["pos"] = pos + ns
        rstate["next_load"] = i + 1
        rstate["loaded"].append((i, pos, ns))
        keys = [f"slot{s}" for s in range(pos, pos + ns)]
        order_rd = [f"dmaorder{i - INFLIGHT}"] if i >= INFLIGHT else []
        if i == 0:
            order_rd = [f"xst{t}" for t in range(6)]
        if ns == 1:
            dst = ring[pos][:, 0:nk * cw].rearrange("p (k n) -> p k n", k=nk)
            p.op("pool", lambda e: e.dma_start(out=dst, in_=src), reads=order_rd, writes=keys + [f"dmaorder{i}"], dsem=f"ring{pos}")
        else:
            hk = nk // 2
            for hh in range(2):
                dst = ring[pos + hh][:, 0:hk * cw].rearrange("p (k n) -> p k n", k=hk)
                srch = src[:, hh * hk:(hh + 1) * hk, :]
                p.op("pool", (lambda e, dst=dst, srch=srch: e.dma_start(out=dst, in_=srch)),
                     reads=order_rd, writes=[keys[hh]] + ([f"dmaorder{i}"] if hh == 1 else []), dsem=f"ring{pos + hh}")
        return True

    def ring_fill():
        while ring_issue_one():
            pass

    class Blk:
        pass

    def ring_get():
        i = rstate["next_get"]
        rstate["next_get"] += 1
        ring_fill()
        ent = [e for e in rstate["loaded"] if e[0] == i]
        assert ent, f"block {i} not loaded"
        _, pos, ns = ent[0]
        src, nk, cw, _ = blocks[i]
        b = Blk()
        b.keys = [f"slot{s}" for s in range(pos, pos + ns)]
        if ns == 1:
            v = ring[pos][:, 0:nk * cw].rearrange("p (k n) -> p k n", k=nk)
            b.w = lambda k, c0, c1: v[:, k, c0:c1]
        else:
            hk = nk // 2
            vs = [ring[pos + hh][:, 0:hk * cw].rearrange("p (k n) -> p k n", k=hk) for hh in range(2)]
            b.w = lambda k, c0, c1: vs[k // hk][:, k % hk, c0:c1]
        return b

    def ring_done(n=1):
        rstate["next_use"] += n
        assert rstate["next_use"] <= rstate["next_get"]
        ring_fill()

    def mm_group(out_ap, out_key, pairs, reads):
        n = len(pairs)

        def emit(e):
            inst = None
            for i, (l, r) in enumerate(pairs):
                inst = e.matmul(out_ap, l, r, start=(i == 0), stop=(i == n - 1))
            return inst
        p.op("pe", emit, reads=reads, writes=[out_key])

    def transposes(out_key, items, reads):
        def emit(e):
            inst = None
            for (o, i_, n) in items:
                inst = e.transpose(o, i_, ident[0:n, 0:n])
            return inst
        p.op("pe", emit, reads=list(reads) + ["ident"], writes=[out_key])

    def act(out, in_, func, reads, writes, bias=None, scale=None, accum_out=None, eng="act"):
        kw = {}
        if bias is not None:
            kw["bias"] = bias
        if scale is not None:
            kw["scale"] = scale
        if accum_out is not None:
            kw["accum_out"] = accum_out
        p.op("act", lambda e: e.activation(out, in_, func, **kw), reads=reads, writes=writes)

    def tt(out, a, b, op, reads, writes, eng="dve"):
        p.op(eng, lambda e: e.tensor_tensor(out, a, b, op), reads=reads, writes=writes)

    def ts(out, a, s1, s2, op0, op1, reads, writes, eng="dve"):
        if op1 is None:
            p.op(eng, lambda e: e.tensor_scalar(out, a, s1, None, op0), reads=reads, writes=writes)
        else:
            p.op(eng, lambda e: e.tensor_scalar(out, a, s1, s2, op0, op1), reads=reads, writes=writes)

    def stt(out, a, s, b, op0, op1, reads, writes):
        p.op("dve", lambda e: e.scalar_tensor_tensor(out, a, s, b, op0, op1), reads=reads, writes=writes)

    def cp(out, in_, reads, writes, eng="dve"):
        if eng == "act":
            p.op("act", lambda e: e.activation(out, in_, AF.Copy), reads=reads, writes=writes)
        else:
            p.op(eng, lambda e: e.tensor_copy(out, in_), reads=reads, writes=writes)

    def dma(q, out, in_, reads, writes, sem):
        p.op(q, lambda e: e.dma_start(out=out, in_=in_), reads=reads, writes=writes, dsem=sem)

    dma("sp", ident[:, :], ident_d[:, :], [], ["ident"], "ld_a")
    setup = contextlib.ExitStack()
    xst = sb(setup, "xst", [128, 6, D])
    for i in range(6):
        p.newsem(f"xl{i}")
    vA = sb(setup, "vA", [96, 128])
    vB = sb(setup, "vB", [84, 128])
    cstT = sb(setup, "cstT", [128, 6, 128])
    rdeT = sb(setup, "rdeT", [128, 8])
    gwt = sb(setup, "gwt", [128, 4, 128])
    tmpM = sb(setup, "tmpM", [128, 2, 128])
    dma("sp", vA[:, :], vecA[:, :], [], ["vA"], "ld_b")
    dma("sp", vB[:, :], vecB[:, :], [], ["vB"], "ld_c")
    dma("sp", cstT[:, :, :], cst[:, :, :], [], ["cstT"], "ld_d")
    dma("sp", rdeT[:, :], rde[:, :], [], ["rdeT"], "ld_e")
    dma("sp", colsT[:, :], cols[:, :], [], ["colsT"], "ld_f")
    dma("sp", gwt[:, :, :], gws.rearrange("g p q -> p g q"), [], ["gwt"], "ld_g")
    for t in range(6):
        dma("sp", xst[:, t, :], xin[t * 128:(t + 1) * 128, :], [], [f"xst{t}"], f"xl{t}")
    ring_fill()
    p.op("dve", lambda e: e.memset(ones_bf[:, :], 1.0), writes=["ones_bf"])
    p.op("dve", lambda e: e.memset(ones_f[:, :], 1.0), writes=["ones_f"])
    transposes(PK[4], [(PS[4][:, 0:96], vA[:, :], 96), (PS[4][:, 128:212], vB[:, :], 84)], ["vA", "vB"])
    cp(vecAT[:, :], PS[4][:, 0:96], [PK[4]], ["vecAT"])
    cp(vecBT[:, :], PS[4][:, 128:212], [PK[4]], ["vecBT"])
    for c in range(2):
        act(scT[:, :, c], vecBT[:, 40 + 8 * c:48 + 8 * c], AF.Silu, ["vecBT"], ["scT"])
    transposes(PK[5], [(PS[5][:, g * 128:(g + 1) * 128], gwt[:, g, :], 128) for g in range(4)], ["gwt"])
    cp(wsT[:, :, :], PS[5][:, :].rearrange("p (g q) -> p g q", g=4), [PK[5]], ["wsT"])
    act(lgB[:, :], rdeT[:, :], AF.Exp, ["rdeT"], ["lgB"], scale=-float(np.log(2.0)))
    act(lgB[:, :], lgB[:, :], AF.Ln, ["lgB"], ["lgB"], scale=-1.0, bias=1.0)
    act(cdB[:, :], lgB[:, :], AF.Exp, ["lgB"], ["cdB"], scale=128.0)
    for d in range(2):
        act(c0B[:, 4 * d:4 * d + 4], lgB[:, 4 * d:4 * d + 4], AF.Exp, ["lgB", "colsT"], ["c0B"], scale=colsT[:, 2 + d:3 + d])
        act(kd4[:, 4 * d:4 * d + 4], lgB[:, 4 * d:4 * d + 4], AF.Exp, ["lgB", "colsT"], ["kd4"], scale=colsT[:, d:d + 1])
        for t in range(6):
            act(kdo[:, t, 4 * d:4 * d + 4], lgB[:, 4 * d:4 * d + 4], AF.Exp, ["lgB", "colsT"], ["kdo"],
                scale=colsT[:, 4 + 2 * t + d:5 + 2 * t + d])
    sdk = float(128 ** -0.5)
    ts(kdo[:, :, :], kdo[:, :, :], sdk, None, OP.mult, None, ["kdo"], ["kdo"])
    for d in range(2):
        for h in range(4):
            ts(KD[:, d, h * 128:(h + 1) * 128], ones_f[:, :], kd4[:, 4 * d + h:4 * d + h + 1], sdk, OP.mult, OP.mult,
               ["ones_f", "kd4"], ["KD"])
            act(qdec[:, d, h, :], cstT[:, 4 + d, :], AF.Exp, ["cstT", "lgB"], ["qdec"], scale=lgB[:, 4 * d + h:4 * d + h + 1])
    for h in range(4):
        act(tmpM[:, 0, :], cstT[:, 0, :], AF.Exp, ["cstT", "lgB"], ["tmpM0"], scale=lgB[:, h:h + 1])
        act(tmpM[:, 1, :], cstT[:, 1, :], AF.Exp, ["cstT", "lgB"], ["tmpM1"], scale=lgB[:, 4 + h:5 + h])
        tt(tmpM[:, 0, :], tmpM[:, 0, :], cstT[:, 2, :], OP.mult, ["tmpM0", "cstT"], ["tmpM0"])
        tt(tmpM[:, 1, :], tmpM[:, 1, :], cstT[:, 3, :], OP.mult, ["tmpM1", "cstT"], ["tmpM1"])
        tt(Mmask[:, h, :], tmpM[:, 0, :], tmpM[:, 1, :], OP.add, ["tmpM0", "tmpM1"], ["Mmask"])
    stage("setup")

    def adaln_gen(l, which):
        for half in range(2):
            b = ring_get()
            mm_group(PS[6][0:2, :], PK[6], [(scT[:, k, :], b.w(k, 0, 512)) for k in range(8)], ["scT"] + b.keys)
            ring_done()
            cp(mst[0:2, :], PS[6][0:2, :], [PK[6]], ["mst"], eng="act")
            transposes(PK[7], [(PS[7][:, 8 * half + 2 * j:8 * half + 2 * j + 2], mst[0:2, j * 128:(j + 1) * 128], 2) for j in range(4)],
                       ["mst"])
            if half == 0:
                yield
        for c in range(2):
            tt(mod[l][:, which * 8:(which + 1) * 8, c], PS[7][:, 0:16].rearrange("p (j c) -> p j c", c=2)[:, :, c],
               vecAT[:, l * 48 + which * 8:l * 48 + which * 8 + 8], OP.add, [PK[7], "vecAT"], [f"mod{l}_{which}"])
        if which in (1, 4):
            ni = 0 if which == 1 else 1
            for c in range(2):
                stt(Amod[:, l * 2 + ni, :, c], mod[l][:, which * 8:(which + 1) * 8, c], 1.0,
                    vecBT[:, (l * 2 + ni) * 8:(l * 2 + ni) * 8 + 8], OP.add, OP.mult,
                    [f"mod{l}_{which}", "vecBT"], [f"Amod{l}{ni}"])
        yield

    def adaln_block(l, which, es=None):
        for _ in adaln_gen(l, which):
            pass

    def chain(*gens):
        for g in gens:
            yield from g

    def zip_emit(main, filler):
        m_done = f_done = False
        while not (m_done and f_done):
            if not m_done:
                try:
                    next(main)
                except StopIteration:
                    m_done = True
            if not f_done:
                try:
                    next(filler)
                except StopIteration:
                    f_done = True

    GRPS = [(0, 512, 0), (512, 768, 1)]

    stat = {"pending": [], "n": 0, "tok": False}
    ACC = PS[5][:, 384:390]

    def stats_chunk(k):
        r = stat["n"] % 2
        stat["n"] += 1
        act(nsq[:, r, :], xT[:, k, :], AF.Square, ["xT"], [f"nsq{r}"])
        first, last = (k == 0), (k == 7)
        tokmode = stat["tok"]

        def pe_part():
            if tokmode:
                def emit(e):
                    inst = None
                    for t in range(6):
                        inst = e.matmul(PS[5][:, 384 + t:385 + t], nsq[:, r, t * 128:(t + 1) * 128], ones_bf[:, 0:1],
                                        start=(first and t == 0), stop=last, skip_group_check=True)
                    return inst
                p.op("pe", emit, reads=["ones_bf", f"nsq{r}"], writes=[PK[5]])
            else:
                for gi, (a, b_, c) in enumerate(GRPS):
                    def emit(e, gi=gi, a=a, b_=b_):
                        return e.matmul(PS[4 + gi][:, 0:b_ - a], ones_bf[:, :], nsq[:, r, a:b_], start=first, stop=last)
                    p.op("pe", emit, reads=["ones_bf", f"nsq{r}"], writes=[PK[4 + gi]])
        stat["pending"].append(pe_part)

    def stats_flush(keep=0):
        while len(stat["pending"]) > keep:
            stat["pending"].pop(0)()

    def stats_finish(final=False):
        stats_flush(0)
        if final:
            act(st6[:, 0:6], ACC, AF.Ln, [PK[5]], ["st6"], scale=1.0 / D, bias=EPS)
            act(st6[:, 8:14], st6[:, 0:6], AF.Exp, ["st6"], ["st6b"], scale=-0.5)
            return
        for gi, (a, b_, c) in enumerate(GRPS):
            ps = PS[4 + gi][:, 0:b_ - a]
            act(ps, ps, AF.Ln, [PK[4 + gi]], [PK[4 + gi]], scale=1.0 / D, bias=EPS)
            act(ps, ps, AF.Exp, [PK[4 + gi]], [PK[4 + gi]], scale=-0.5)

    def modulate_gen(l, ni):
        for k in range(8):
            for gi, (a, b_, c) in enumerate(GRPS):
                tt(ntmp[:, k % 2, a:b_], xT[:, k, a:b_], PS[4 + gi][:, 0:b_ - a], OP.mult, ["xT", PK[4 + gi]], [f"ntmp{k % 2}{gi}"])
            for gi, (a, b_, c) in enumerate(GRPS):
                rd = [f"ntmp{k % 2}{gi}", f"Amod{l}{ni}", f"mod{l}_{ni * 3}"]
                if gi == 0:
                    act(hT[:, k, a:b_], ntmp[:, k % 2, a:b_], AF.Identity, rd, ["hT0"],
                        scale=Amod[:, l * 2 + ni, k, c:c + 1], bias=mod[l][:, ni * 24 + k, c:c + 1])
                else:
                    ts(hT[:, k, a:b_], ntmp[:, k % 2, a:b_], Amod[:, l * 2 + ni, k, c:c + 1], mod[l][:, ni * 24 + k, c:c + 1],
                       OP.mult, OP.add, rd, ["hT1"])
            yield

    def modulate(l, ni, filler=None):
        zip_emit(modulate_gen(l, ni), filler if filler is not None else iter(()))

    def xload():
        for t in range(6):
            for half in range(2):
                bk = 4 + (2 * t + half) % 2
                transposes(PK[bk], [(PS[bk][:, j * 128:(j + 1) * 128], xst[:, t, (half * 4 + j) * 128:(half * 4 + j + 1) * 128], 128)
                                    for j in range(4)], [f"xst{t}"])
                cp(xT[:, half * 4:half * 4 + 4, t * 128:(t + 1) * 128], PS[bk][:, :].rearrange("p (j n) -> p j n", j=4),
                   [PK[bk]], ["xT"], eng=("act" if half == 0 else "dve"))

    L0 = contextlib.ExitStack()
    xload()
    stage("xload")
    for k in range(8):
        stats_chunk(k)
        stats_flush(1)
    stats_flush(0)
    adaln_block(0, 0, L0)
    adaln_block(0, 1, L0)
    stage("adaln0")
    stats_finish()
    p.soft_barrier()
    setup.close()
    modulate(0, 0)
    stage("norm0")

    P1 = contextlib.ExitStack()
    onorm = sb(P1, "onorm", [128, 6, 1024], BF16)

    PA = contextlib.ExitStack()
    q_tok = sb(PA, "q_tok", [128, 512])
    k_tok = sb(PA, "k_tok", [128, 512])
    ksc2 = sb(PA, "ksc", [128, 2, 2, 2, 512], BF16)
    v_tok2 = sb(PA, "v_tok", [128, 2, 2, 1024], BF16)
    qT2 = sb(PA, "qT", [128, 2, 4, 256], BF16)
    kT2 = sb(PA, "kT", [128, 2, 4, 256], BF16)
    qdT2 = sb(PA, "qdT", [128, 2, 2, 4, 256], BF16)
    KSC = [ksc2[:, i] for i in range(2)]
    VT = [v_tok2[:, i] for i in range(2)]
    QT = [qT2[:, i] for i in range(2)]
    KT = [kT2[:, i] for i in range(2)]
    QDT = [qdT2[:, i] for i in range(2)]
    PT = sb(PA, "PT", [128, 2, 4, 128], BF16)
    Sfp = sb(PA, "Sfp", [128, 8, 256])
    Sin_bf = sb(PA, "Sin_bf", [128, 8, 256], BF16)
    S1_bf = sb(PA, "S1_bf", [128, 8, 256], BF16)
    junk = sb(PA, "junk", [128, 2, 256], BF16)
    ss4 = sb(PA, "ss4", [128, 2, 4])
    xo = sb(PA, "xo", [128, 2, D])
    dgo = sb(PA, "dgo", [128, 2, 128])
    junk2 = sb(PA, "junk2", [128, D], BF16)
    hoT = sb(PA, "hoT", [128, 2, 8, 128], BF16)
    osm = sb(PA, "osm", [128, 2, 4])
    kso = sb(PA, "kso", [128, 2, 512], BF16)
    vo = sb(PA, "vo", [128, 1024], BF16)
    SinF = sb(PA, "SinF", [128, 8, 256])

    wq = ring_get()
    wk = ring_get()
    wv0 = ring_get()
    wv1 = ring_get()
    psrot = {"i": 0}

    def nextbank():
        b = psrot["i"] % 4
        psrot["i"] += 1
        return b

    def inproj(lhs_fn, lhs_keys, blk, evac):
        bk = nextbank()
        mm_group(PS[bk][:, :], PK[bk], [(lhs_fn(k), blk.w(k, 0, 512)) for k in range(8)], lhs_keys + blk.keys)
        evac(bk)

    def passA_tile(t, tt_i, sq):
        ksc, v_tok, qT, kT, qdT = KSC[sq], VT[sq], QT[sq], KT[sq], QDT[sq]
        cs = slice(t * 128, (t + 1) * 128)
        lhs = lambda k: hT[:, k, cs]
        def ev_q(bk):
            cp(q_tok[:, :], PS[bk][:, :], [PK[bk]], ["q_tok"], eng="act")
        inproj(lhs, ["hT0", "hT1"], wq, ev_q)
        yield
        def ev_k(bk):
            cp(k_tok[:, :], PS[bk][:, :], [PK[bk]], ["k_tok"], eng="act")
            for d in range(2):
                tt(ksc[:, tt_i, d, :], PS[bk][:, :], KD[:, d, :], OP.mult, [PK[bk], "KD"], [f"ksc{sq}{tt_i}"])
        inproj(lhs, ["hT0", "hT1"], wk, ev_k)
        yield
        for hv, wv_ in enumerate((wv0, wv1)):
            def ev_v(bk, hv=hv):
                cp(v_tok[:, tt_i, hv * 512:(hv + 1) * 512], PS[bk][:, :], [PK[bk]], [f"v{sq}{tt_i}h{hv}"],
                   eng=("act" if hv == 0 else "dve"))
            inproj(lhs, ["hT0", "hT1"], wv_, ev_v)
            yield
        bq = nextbank()
        transposes(PK[bq], [(PS[bq][:, h * 128:(h + 1) * 128], q_tok[:, h * 128:(h + 1) * 128], 128) for h in range(4)], ["q_tok"])
        ls = slice(tt_i * 128, (tt_i + 1) * 128)
        psqv = PS[bq][:, :].rearrange("p (h n) -> p h n", h=4)
        cp(qT[:, :, ls], psqv, [PK[bq]], [f"qT{sq}{tt_i}"], eng="act")
        for d in range(2):
            tt(qdT[:, d, :, ls], psqv, qdec[:, d, :, :], OP.mult, [PK[bq], "qdec"], [f"qdT{sq}{tt_i}"])
        bkk = nextbank()
        transposes(PK[bkk], [(PS[bkk][:, h * 128:(h + 1) * 128], k_tok[:, h * 128:(h + 1) * 128], 128) for h in range(4)], ["k_tok"])
        cp(kT[:, :, ls], PS[bkk][:, :].rearrange("p (h n) -> p h n", h=4), [PK[bkk]], [f"kT{sq}{tt_i}"], eng="act")
        yield

    Sfp4 = Sfp[:, :, :].rearrange("p (d h) e -> p d h e", d=2)
    S1_bf4 = S1_bf[:, :, :].rearrange("p (d h) e -> p d h e", d=2)

    def retention(seq, is_sample):
        sq = seq % 2
        ksc, v_tok, qT, kT, qdT = KSC[sq], VT[sq], QT[sq], KT[sq], QDT[sq]
        for c in range(2):
            ls = slice(c * 128, (c + 1) * 128)
            bk = 4 + c
            def emit(e, ls=ls, bk=bk):
                inst = None
                for h in range(4):
                    inst = e.matmul(PS[bk][:, h * 128:(h + 1) * 128], kT[:, h, ls], qT[:, h, ls], start=True, stop=True)
                return inst
            p.op("pe", emit, reads=[f"kT{sq}{c}", f"qT{sq}{c}"], writes=[PK[bk]])
            tt(PT[:, c, :, :], PS[bk][:, :].rearrange("p (h n) -> p h n", h=4), Mmask[:, :, :], OP.mult,
               [PK[bk], "Mmask"], [f"PT{c}"])
            yield
        for h in range(4):
            bk = 6 + (h % 2)
            def emit(e, h=h, bk=bk):
                e.matmul(PS[bk][:, 0:256], ksc[:, 0, 0, h * 128:(h + 1) * 128], v_tok[:, 0, h * 256:(h + 1) * 256], start=True, stop=True)
                return e.matmul(PS[bk][:, 256:512], ksc[:, 1, 1, h * 128:(h + 1) * 128], v_tok[:, 1, h * 256:(h + 1) * 256], start=True, stop=True)
            p.op("pe", emit, reads=[f"ksc{sq}0", f"ksc{sq}1", f"v{sq}0h0", f"v{sq}0h1", f"v{sq}1h0", f"v{sq}1h1"], writes=[PK[bk]])
            if is_sample:
                for d in range(2):
                    stt(S1_bf[:, 4 * d + h, :], SinF[:, 4 * d + h, :], cdB[:, 4 * d + h:4 * d + h + 1],
                        PS[bk][:, d * 256:(d + 1) * 256], OP.mult, OP.add, ["SinF", "cdB", PK[bk]], ["S1_bf"])
            else:
                cp(Sfp4[:, :, h, :], PS[bk][:, :].rearrange("p (d e) -> p d e", d=2), [PK[bk]], ["Sfp"], eng="act")
                cp(S1_bf4[:, :, h, :], Sfp4[:, :, h, :], ["Sfp"], ["S1_bf"], eng="dve")
            yield
        for c in range(2):
            ls = slice(c * 128, (c + 1) * 128)
            t = 2 * seq + c
            for hp in range(2):
                bk = 4 + hp if c == 0 else 6 + hp
                def emit(e, c=c, hp=hp, bk=bk, ls=ls):
                    inst = None
                    for hh in range(2):
                        h = hp * 2 + hh
                        o_ap = PS[bk][:, hh * 256:(hh + 1) * 256]
                        terms = [(PT[:, c, h, :], v_tok[:, c, h * 256:(h + 1) * 256])]
                        if c == 0:
                            if is_sample:
                                terms.append((qdT[:, 0, h, ls], Sin_bf[:, h, :]))
                            terms.append((qdT[:, 1, h, ls], S1_bf[:, 4 + h, :]))
                        else:
                            terms.append((qdT[:, 0, h, ls], S1_bf[:, h, :]))
                            if is_sample:
                                terms.append((qdT[:, 1, h, ls], Sin_bf[:, 4 + h, :]))
                        for i, (l_, r_) in enumerate(terms):
                            inst = e.matmul(o_ap, l_, r_, start=(i == 0), stop=(i == len(terms) - 1))
                    return inst
                p.op("pe", emit, reads=[f"PT{c}", f"v{sq}{c}h0", f"v{sq}{c}h1", f"qdT{sq}{c}", "Sin_bf", "S1_bf"], writes=[PK[bk]])
                for hh in range(2):
                    h = hp * 2 + hh
                    act(junk[:, hh, :], PS[bk][:, hh * 256:(hh + 1) * 256], AF.Square, [PK[bk]], [f"junk{hh}", f"ss4{c}"],
                        accum_out=ss4[:, c, h:h + 1])
                yield
            act(ss4[:, c, :], ss4[:, c, :], AF.Ln, [f"ss4{c}"], [f"ss4{c}"], scale=1.0 / 256.0, bias=EPS)
            act(ss4[:, c, :], ss4[:, c, :], AF.Exp, [f"ss4{c}"], [f"ss4{c}"], scale=-0.5)
            for hp in range(2):
                bk = 4 + hp if c == 0 else 6 + hp
                for hh in range(2):
                    h = hp * 2 + hh
                    ts(onorm[:, t, h * 256:(h + 1) * 256], PS[bk][:, hh * 256:(hh + 1) * 256], ss4[:, c, h:h + 1], 0.5,
                       OP.mult, OP.mult, [PK[bk], f"ss4{c}"], [f"onorm{t}"])
            yield
        if not is_sample:
            for h in range(4):
                bk = 4 + (h % 2)
                def emit(e, h=h, bk=bk):
                    e.matmul(PS[bk][:, 0:256], ksc[:, 1, 0, h * 128:(h + 1) * 128], v_tok[:, 1, h * 256:(h + 1) * 256], start=True, stop=True)
                    return e.matmul(PS[bk][:, 256:512], ksc[:, 0, 1, h * 128:(h + 1) * 128], v_tok[:, 0, h * 256:(h + 1) * 256], start=True, stop=True)
                p.op("pe", emit, reads=[f"ksc{sq}0", f"ksc{sq}1", f"v{sq}0h0", f"v{sq}0h1", f"v{sq}1h0", f"v{sq}1h1"], writes=[PK[bk]])
                for d in range(2):
                    stt(Sfp[:, 4 * d + h, :], Sfp[:, 4 * d + h, :], cdB[:, 4 * d + h:4 * d + h + 1],
                        PS[bk][:, d * 256:(d + 1) * 256], OP.mult, OP.add, ["Sfp", "cdB", PK[bk]], ["Sfp"])
                yield
            for d in range(2):
                dma("sp", ns_out[seq, d].rearrange("h d e -> d h e"), Sfp[:, 4 * d:4 * d + 4, :], ["Sfp"], [], "st_ns")

    def others_A(t):
        r = t % 2
        dma("sp", xo[:, r, :], xoth[t * 128:(t + 1) * 128, :], [], [f"xo{r}"], f"xs{r}")
        act(junk2[:, :], xo[:, r, :], AF.Square, [f"xo{r}"], ["junk2", f"osm{r}"], accum_out=osm[:, r, 0:1])
        act(osm[:, r, 1:2], osm[:, r, 0:1], AF.Ln, [f"osm{r}"], [f"osmb{r}"], scale=1.0 / D, bias=EPS)
        act(osm[:, r, 2:3], osm[:, r, 1:2], AF.Exp, [f"osmb{r}"], [f"osmc{r}"], scale=-0.5)
        ts(dgo[:, r, :], ident[:, :], osm[:, r, 2:3], None, OP.mult, None, ["ident", f"osmc{r}"], [f"dgo{r}"])

    def others_B1(t):
        r = t % 2
        for half in range(2):
            bk = nextbank()
            def emit(e, half=half, bk=bk, r=r):
                inst = None
                for j in range(4):
                    k = half * 4 + j
                    inst = e.matmul(PS[bk][:, j * 128:(j + 1) * 128], xo[:, r, k * 128:(k + 1) * 128], dgo[:, r, :],
                                    start=True, stop=True)
                return inst
            p.op("pe", emit, reads=[f"xo{r}", f"dgo{r}"], writes=[PK[bk]])
            for j in range(4):
                k = half * 4 + j
                if half == 0:
                    ts(hoT[:, r, k, :], PS[bk][:, j * 128:(j + 1) * 128], Amod[:, 0, k, 1:2], mod[0][:, k, 1:2], OP.mult, OP.add,
                       [PK[bk], "Amod00", "mod0_0"], [f"hoT{r}a"])
                else:
                    act(hoT[:, r, k, :], PS[bk][:, j * 128:(j + 1) * 128], AF.Identity, [PK[bk], "Amod00", "mod0_0"], [f"hoT{r}b"],
                        scale=Amod[:, 0, k, 1:2], bias=mod[0][:, k, 1:2])

    def others_B2(t):
        r = t % 2
        lhs = lambda k: hoT[:, r, k, :]
        def ev_ko(bk, t=t):
            for h in range(4):
                ts(kso[:, 0, h * 128:(h + 1) * 128], PS[bk][:, h * 128:(h + 1) * 128], kdo[:, t, h:h + 1], None,
                   OP.mult, None, [PK[bk], "kdo"], ["kso0"])
            for h in range(4):
                act(kso[:, 1, h * 128:(h + 1) * 128], PS[bk][:, h * 128:(h + 1) * 128], AF.Copy, [PK[bk], "kdo"], ["kso1"],
                    scale=kdo[:, t, 4 + h:5 + h])
        inproj(lhs, [f"hoT{r}a", f"hoT{r}b"], wk, ev_ko)
        for hv, wv_ in enumerate((wv0, wv1)):
            def ev_vo(bk, hv=hv):
                cp(vo[:, hv * 512:(hv + 1) * 512], PS[bk][:, :], [PK[bk]], [f"vo{hv}"], eng=("act" if hv == 0 else "dve"))
            inproj(lhs, [f"hoT{r}a", f"hoT{r}b"], wv_, ev_vo)

    def others_B3(t):
        def emit(e, t=t):
            inst = None
            for h in range(4):
                bk = 4 + h
                e.matmul(PS[bk][:, 0:256], kso[:, 0, h * 128:(h + 1) * 128], vo[:, h * 256:(h + 1) * 256],
                         start=(t == 0), stop=(t == 5), skip_group_check=True)
                inst = e.matmul(PS[bk][:, 256:512], kso[:, 1, h * 128:(h + 1) * 128], vo[:, h * 256:(h + 1) * 256],
                                start=False, stop=(t == 5), skip_group_check=True)
            return inst
        p.op("pe", emit, reads=["kso0", "kso1", "vo0", "vo1"], writes=[PK[4], PK[5], PK[6], PK[7]])
        if t == 5:
            SinF4 = SinF[:, :, :].rearrange("p (d h) e -> p d h e", d=2)
            for h in range(4):
                bk = 4 + h
                tt(SinF4[:, :, h, :], SinF4[:, :, h, :], PS[bk][:, :].rearrange("p (d e) -> p d e", d=2), OP.add,
                   ["SinF", PK[bk]], ["SinF"])

    def interleave(main, fillers):
        fillers = list(fillers)
        main_done = False
        while not main_done or fillers:
            if not main_done:
                try:
                    next(main)
                except StopIteration:
                    main_done = True
            if fillers:
                try:
                    next(fillers[0])
                except StopIteration:
                    fillers.pop(0)

    dma("sp", SinF[:, :, :], s0.rearrange("a h d e -> d (a h) e"), [], ["SinF"], "ld_h")
    for j in range(8):
        ts(SinF[:, j, :], SinF[:, j, :], c0B[:, j:j + 1], None, OP.mult, None, ["SinF", "c0B"], ["SinF"])

    def seq_inproj(seq):
        for tt_i in range(2):
            yield from passA_tile(2 * seq + tt_i, tt_i, seq % 2)

    for _ in seq_inproj(0):
        pass
    zip_emit(retention(0, False), seq_inproj(1))
    stage("ret0")
    zip_emit(retention(1, False), seq_inproj(2))
    stage("ret1")
    others_A(0)
    others_A(1)
    others_B1(0)
    for t in range(6):
        if t + 2 < 6:
            others_A(t + 2)
        others_B2(t)
        if t + 1 < 6:
            others_B1(t + 1)
        others_B3(t)
    cp(Sin_bf[:, :, :], SinF[:, :, :], ["SinF"], ["Sin_bf"], eng="dve")
    stage("others")
    for _ in retention(2, True):
        pass
    ring_done(4)
    p.soft_barrier()
    PA.close()
    stage("passA")
    P1b = contextlib.ExitStack()
    mixT = sb(P1b, "mixT", [128, 12, NT], BF16)

    PB = contextlib.ExitStack()
    gtmp = sb(PB, "gtmp", [128, 2, 512])
    mix_tok = sb(PB, "mix_tok", [128, 2, 1536])
    u_tok = sb(PB, "u_tok", [128, 2, 512])
    zg = sb(PB, "zg", [128, 2, 512])
    zn = sb(PB, "zn", [128, 2, 512], BF16)
    bst = sb(PB, "bst", [128, 2, 16])
    mhalf = sb(PB, "mhalf", [128, 2])
    p.op("dve", lambda e: e.memset(mhalf[:, :], -0.5), writes=["mhalf"])
    wz = ring_get()
    wu = ring_get()
    wg0 = ring_get()
    wg1 = ring_get()
    def passB_zu(t):
        r = t % 2
        cs = slice(t * 128, (t + 1) * 128)
        lhs = lambda k, cs=cs: hT[:, k, cs]
        def ev_z(bk, r=r):
            act(zg[:, r, :], PS[bk][:, :], GELU, [PK[bk]], [f"zg{r}"])
            p.op("dve", lambda e: e.bn_stats(bst[:, r, 0:6], zg[:, r, :]), reads=[f"zg{r}"], writes=[f"bst{r}"])
            p.op("dve", lambda e: e.bn_aggr(bst[:, r, 8:10], bst[:, r, 0:6]), reads=[f"bst{r}"], writes=[f"bstb{r}"])
            ts(bst[:, r, 10:11], bst[:, r, 9:10], EPS, None, OP.add, None, [f"bstb{r}"], [f"bstc{r}"])
            p.op("pool", lambda e: e.tensor_tensor(bst[:, r, 11:12], bst[:, r, 10:11], mhalf[:, 0:1], OP.pow),
                 reads=[f"bstc{r}", "mhalf"], writes=[f"bstd{r}"])
            ts(zn[:, r, :], zg[:, r, :], bst[:, r, 8:9], bst[:, r, 11:12], OP.subtract, OP.mult,
               [f"zg{r}", f"bstb{r}", f"bstd{r}"], [f"zn{r}"])
        inproj(lhs, ["hT0", "hT1"], wz, ev_z)
        def ev_u(bk, r=r):
            act(u_tok[:, r, :], PS[bk][:, :], GELU, [PK[bk]], [f"u_tok{r}"])
        inproj(lhs, ["hT0", "hT1"], wu, ev_u)

    def passB_sv(t):
        r = t % 2
        def emit(e, r=r):
            inst = None
            for g in range(4):
                inst = e.matmul(PS[4][:, g * 128:(g + 1) * 128], wsT[:, g, :], zn[:, r, g * 128:(g + 1) * 128], start=True, stop=True)
            return inst
        p.op("pe", emit, reads=["wsT", f"zn{r}"], writes=[PK[4]])
        for g in range(4):
            stt(mix_tok[:, r, 1024 + g * 128:1024 + (g + 1) * 128], PS[4][:, g * 128:(g + 1) * 128], vecBT[:, 80 + g:81 + g],
                u_tok[:, r, g * 128:(g + 1) * 128], OP.add, OP.mult, [PK[4], "vecBT", f"u_tok{r}"], [f"mix_gm{r}"])

    def passB_g(t):
        r = t % 2
        cs = slice(t * 128, (t + 1) * 128)
        lhs = lambda k, cs=cs: hT[:, k, cs]
        for gi, wg in enumerate((wg0, wg1)):
            def ev_g(bk, gi=gi, t=t, r=r):
                act(gtmp[:, gi, :], PS[bk][:, :], AF.Tanh, [PK[bk]], [f"gtmp{gi}"], scale=0.5)
                stt(gtmp[:, gi, :], gtmp[:, gi, :], 1.0, PS[bk][:, :], OP.add, OP.mult, [f"gtmp{gi}", PK[bk]], [f"gtmp{gi}"])
                tt(mix_tok[:, r, gi * 512:(gi + 1) * 512], gtmp[:, gi, :], onorm[:, t, gi * 512:(gi + 1) * 512], OP.mult,
                   [f"gtmp{gi}", f"onorm{t}"], [f"mix_g{gi}{r}"])
            inproj(lhs, ["hT0", "hT1"], wg, ev_g)

    def passB_T(t):
        r = t % 2
        cs = slice(t * 128, (t + 1) * 128)
        for j3 in range(3):
            bk = 5 + j3
            rk = [f"mix_g0{r}", f"mix_g1{r}", f"mix_gm{r}"][j3] if j3 < 2 else f"mix_gm{r}"
            transposes(PK[bk], [(PS[bk][:, j * 128:(j + 1) * 128], mix_tok[:, r, (j3 * 4 + j) * 128:(j3 * 4 + j + 1) * 128], 128)
                                for j in range(4)], [rk])
            cp(mixT[:, j3 * 4:j3 * 4 + 4, cs], PS[bk][:, :].rearrange("p (j n) -> p j n", j=4), [PK[bk]], [f"mixT{j3}"],
               eng=("act" if j3 % 2 == 0 else "dve"))

    passB_zu(0)
    passB_sv(0)
    passB_g(0)
    for t in range(6):
        if t + 1 < 6:
            passB_zu(t + 1)
        passB_T(t)
        if t + 1 < 6:
            passB_sv(t + 1)
            passB_g(t + 1)
    ring_done(4)
    p.soft_barrier()
    PB.close()
    stage("passB")

    post_mm = []

    def wproj(blk, nk, m_lo, m_hi, rhs_fn, rhs_keys, evac, bank_base=0):
        for m in range(m_lo, m_hi):
            bi = nextbank() % 2
            for gi, (a, b_, c) in enumerate(GRPS):
                bk = bi * 2 + gi
                mm_group(PS[bk][:, 0:b_ - a], PK[bk], [(blk.w(k, m * 128, (m + 1) * 128), rhs_fn(k, a, b_)) for k in range(nk)],
                         rhs_keys + blk.keys)
                while post_mm:
                    post_mm.pop(0)()
                evac(m, gi, a, b_, c, PS[bk][:, 0:b_ - a], PK[bk])

    def resid_evac(l, which):
        def ev(mg, gi, a, b_, c, ps, pk):
            stt(xT[:, mg, a:b_], ps, mod[l][:, which * 8 + mg, c:c + 1], xT[:, mg, a:b_], OP.mult, OP.add,
                [pk, f"mod{l}_{which}", "xT"], ["xT"])
            if gi == 1:
                stats_flush(0)
                stats_chunk(mg)
        return ev

    adaln_block(0, 2)
    def wout0_steps():
        for bi_ in range(4):
            b = ring_get()
            ev = resid_evac(0, 2)
            wproj(b, 12, 0, 2, lambda k, a, b_: mixT[:, k, a:b_], ["mixT0", "mixT1", "mixT2"],
                  lambda m, gi, a, b_, c, ps, pk, bi_=bi_: ev(bi_ * 2 + m, gi, a, b_, c, ps, pk))
            ring_done()
            yield
    zip_emit(wout0_steps(), chain(adaln_gen(0, 3), adaln_gen(0, 4)))
    stats_finish()
    p.soft_barrier()
    P1b.close()
    P1.close()
    stage("wout0")

    def ffn(l):
        if l == 0:
            modulate(l, 1, chain(adaln_gen(0, 5), adaln_gen(1, 0), adaln_gen(1, 1)))
        else:
            modulate(l, 1, adaln_gen(1, 5))
        F = contextlib.ExitStack()
        aT = sb(F, "aT", [128, 32, NT], BF16)
        rtmp = sb(F, "rtmp", [128, 2, NT])
        cnt = {"i": 0}
        for bi_ in range(8):
            b = ring_get()
            def ev1(m, gi, a, b_, c, ps, pk, bi_=bi_):
                mg = bi_ * 4 + m
                r = cnt["i"] % 2
                cnt["i"] += 1
                act(rtmp[:, r, a:b_], ps, AF.Relu, [pk], [f"rtmp{r}"])
                tt(aT[:, mg, a:b_], rtmp[:, r, a:b_], ps, OP.mult, [f"rtmp{r}", pk], ["aT"])
            wproj(b, 8, 0, 4, lambda k, a, b_: hT[:, k, a:b_], ["hT0", "hT1"], ev1)
            ring_done()
        stat["tok"] = (l == 1)
        for bi_ in range(4):
            b = ring_get()
            ev = resid_evac(l, 5)
            wproj(b, 32, 0, 2, lambda k, a, b_: aT[:, k, a:b_], ["aT"],
                  lambda m, gi, a, b_, c, ps, pk, bi_=bi_: ev(bi_ * 2 + m, gi, a, b_, c, ps, pk))
            ring_done()
        stats_finish(final=(l == 1))
        p.soft_barrier()
        F.close()

    ffn(0)
    stage("ffn0")

    L1 = contextlib.ExitStack()
    modulate(1, 0, adaln_gen(1, 2))
    M1 = contextlib.ExitStack()
    bgT = sb(M1, "bgT", [128, 8, NT])
    cgT = sb(M1, "cgT", [128, 8, NT])
    xc = sb(M1, "xc", [128, 2, NT], BF16)
    mix1T = sb(M1, "mix1T", [128, 8, NT], BF16)
    dg = sb(M1, "dg", [128, 24, 128], BF16)
    for j in range(24):
        ts(dg[:, j, :], ident[:, :], vecBT[:, 56 + j:57 + j], None, OP.mult, None, ["ident", "vecBT"], ["dg"])
    SEGS = [(2, 256), (4, 64)]
    conv_pending = []
    for part in range(3):
        for bi_ in range(2):
            b = ring_get()
            def ev(m, gi, a, b_, c, ps, pk, part=part, bi_=bi_):
                mg = bi_ * 4 + m
                if part == 0:
                    cp(bgT[:, mg, a:b_], ps, [pk], ["bgT"], eng="act")
                elif part == 1:
                    cp(cgT[:, mg, a:b_], ps, [pk], ["cgT"], eng="act")
                else:
                    r = mg % 2
                    tt(xc[:, r, a:b_], cgT[:, mg, a:b_], ps, OP.mult, ["cgT", pk], [f"xc{r}{gi}"])
                    bk = 4 + 2 * r + gi
                    nseg, L = SEGS[gi]
                    n = b_ - a
                    def emit(e, mg=mg, r=r, a=a, b_=b_, n=n, L=L, bk=bk, nseg=nseg):
                        inst = e.matmul(PS[bk][:, 0:n], dg[:, 8 + mg, :], xc[:, r, a:b_], start=True, stop=False)
                        for sgi in range(nseg):
                            o0 = sgi * L
                            last = (sgi == nseg - 1)
                            e.matmul(PS[bk][:, o0 + 1:o0 + L], dg[:, mg, :], xc[:, r, a + o0:a + o0 + L - 1], start=False, stop=False)
                            inst = e.matmul(PS[bk][:, o0:o0 + L - 1], dg[:, 16 + mg, :], xc[:, r, a + o0 + 1:a + o0 + L],
                                            start=False, stop=last)
                        return inst
                    def later(emit=emit, r=r, gi=gi, bk=bk, mg=mg, a=a, b_=b_, n=n):
                        p.op("pe", emit, reads=["dg", f"xc{r}{gi}"], writes=[PK[bk]])
                        tt(mix1T[:, mg, a:b_], bgT[:, mg, a:b_], PS[bk][:, 0:n], OP.mult, ["bgT", PK[bk]], ["mix1T"])
                    post_mm.append(later)
            wproj(b, 8, 0, 4, lambda k, a, b_: hT[:, k, a:b_], ["hT0", "hT1"], ev)
            ring_done()
    while post_mm:
        post_mm.pop(0)()
    def wout1_steps():
        for bi_ in range(2):
            b = ring_get()
            ev = resid_evac(1, 2)
            wproj(b, 8, 0, 4, lambda k, a, b_: mix1T[:, k, a:b_], ["mix1T"],
                  lambda m, gi, a, b_, c, ps, pk, bi_=bi_: ev(bi_ * 4 + m, gi, a, b_, c, ps, pk))
            ring_done()
            yield
    zip_emit(wout1_steps(), chain(adaln_gen(1, 3), adaln_gen(1, 4)))
    stats_finish()
    p.soft_barrier()
    M1.close()
    stage("mix1")
    ffn(1)
    stage("ffn1")

    FN = contextlib.ExitStack()
    dfn = sb(FN, "dfn", [128, 8, 128])
    ytok = [sb(FN, f"ytok{i}", [128, D]) for i in range(3)]
    p.newsem("st_y2")
    for k in range(8):
        ts(dfn[:, k, :], ident[:, :], vecBT[:, 32 + k:33 + k], None, OP.mult, None, ["ident", "vecBT"], ["dfn"])
    for t in range(6):
        for half in range(2):
            bk = (2 * t + half) % 4
            def emit(e, t=t, half=half, bk=bk):
                inst = None
                for j in range(4):
                    k = half * 4 + j
                    inst = e.matmul(PS[bk][:, j * 128:(j + 1) * 128], xT[:, k, t * 128:(t + 1) * 128], dfn[:, k, :],
                                    start=True, stop=True)
                return inst
            p.op("pe", emit, reads=["xT", "dfn"], writes=[PK[bk]])
            if half == 0:
                act(ytok[t % 3][:, 0:512], PS[bk][:, :], AF.Copy, [PK[bk], "st6b"], [f"ytok{t % 3}a"], scale=st6[:, 8 + t:9 + t])
            else:
                ts(ytok[t % 3][:, 512:1024], PS[bk][:, :], st6[:, 8 + t:9 + t], None, OP.mult, None, [PK[bk], "st6b"], [f"ytok{t % 3}b"])
        dma("sp", y_out[t * 128:(t + 1) * 128, :], ytok[t % 3][:, :], [f"ytok{t % 3}a", f"ytok{t % 3}b"], [], f"st_y{t % 3}")
    p.final_wait("sp")
    FN.close()
    L1.close()
    L0.close()
    top.close()
    assert rstate["next_use"] == len(blocks), (rstate["next_use"], len(blocks))


_CACHE = {}


def _consts():
    j = np.arange(128, dtype=np.float32)[:, None]
    i = np.arange(128, dtype=np.float32)[None, :]
    sdk = np.float32(128 ** -0.5)
    cst = np.zeros((128, 6, 128), np.float32)
    cst[:, 0] = np.maximum(i - j, 0)
    cst[:, 1] = np.maximum(j - i, 0)
    cst[:, 2] = (i >= j).astype(np.float32) * sdk
    cst[:, 3] = (j >= i).astype(np.float32) * sdk
    cst[:, 4] = np.broadcast_to(i + 1, (128, 128))
    cst[:, 5] = np.broadcast_to(128 - i, (128, 128))
    return cst


def kernel(**inp):
    f = lambda k: np.ascontiguousarray(np.asarray(inp[k], dtype=np.float32))
    if "nc" not in _CACHE:
        _CACHE["nc"] = build_program()
    nc = _CACHE["nc"]
    x_prompt, x_sample = f("x_prompt"), f("x_sample")
    state, c, c_ctx = f("state_l0_ret"), f("c"), f("c_ctx")
    cst = _consts()
    ident = np.eye(128, dtype=np.float32)
    vecA = np.concatenate([f("l0_b_mod").reshape(48, 128), f("l1_b_mod").reshape(48, 128)], 0)
    rde = np.ascontiguousarray(np.broadcast_to(f("l0_ret_decay_exp").reshape(1, 8), (128, 8)))
    shared = {
        "vecA": vecA, "cst": cst, "rde": rde, "ident_d": ident, "gws": f("l0_gmlp_ws"),
        "l0_w_in": f("l0_w_in"), "l0_w_out": f("l0_w_out"), "l1_w_in": f("l1_w_in"), "l1_w_out": f("l1_w_out"),
    }
    for l in (0, 1):
        shared[f"l{l}_w_mod"] = f(f"l{l}_w_mod")
        shared[f"l{l}_ffn_w1"] = f(f"l{l}_ffn_w1")
        shared[f"l{l}_ffn_w2"] = f(f"l{l}_ffn_w2")
    in_maps = []
    pos = np.arange(128, dtype=np.float32)
    for core in range(8):
        b, q = core // 4, core % 4
        own = slice(256 * q, 256 * q + 256)
        xin = np.concatenate([x_prompt[2 * core], x_prompt[2 * core + 1], x_sample[b, own]], 0)
        oth_idx = np.concatenate([np.arange(0, 256 * q), np.arange(256 * q + 256, 1024)])
        xoth = x_sample[b, oth_idx]
        vecB = np.concatenate([
            f("l0_norm1").reshape(8, 128), f("l0_norm2").reshape(8, 128), f("l1_norm1").reshape(8, 128),
            f("l1_norm2").reshape(8, 128), f("final_norm").reshape(8, 128), c_ctx.reshape(8, 128), c[b].reshape(8, 128),
            f("l1_conv_w").reshape(24, 128), f("l0_gmlp_b").reshape(4, 128)], 0)
        cols = np.zeros((128, 16), np.float32)
        cols[:, 0] = 127 - pos
        cols[:, 1] = pos
        cols[:, 2] = 256 * q
        cols[:, 3] = 768 - 256 * q
        T = oth_idx.astype(np.float32).reshape(6, 128)
        tstart, tend = 256 * q, 256 * q + 255
        BIG = np.float32(1e6)
        for t in range(6):
            ef = np.where(T[t] < tstart, tstart - 1 - T[t], BIG)
            eb = np.where(T[t] > tend, T[t] - tend - 1, BIG)
            cols[:, 4 + 2 * t] = ef
            cols[:, 5 + 2 * t] = eb
        m = dict(shared)
        m.update({"xin": np.ascontiguousarray(xin), "xoth": np.ascontiguousarray(xoth), "s0": np.ascontiguousarray(state[b]),
                  "vecB": np.ascontiguousarray(vecB), "cols": cols})
        in_maps.append(m)
    res = run_bass_kernel_spmd(nc, in_maps, core_ids=list(range(8)))
    y_prompt = np.zeros((16, 256, D), np.float32)
    y_sample = np.zeros((2, 1024, D), np.float32)
    new_state = np.zeros((16, 2, 4, 128, 256), np.float32)
    for core in range(8):
        r = res.results[core]
        b, q = core // 4, core % 4
        y = np.asarray(r["y"])
        y_prompt[2 * core] = y[0:256]
        y_prompt[2 * core + 1] = y[256:512]
        y_sample[b, 256 * q:256 * q + 256] = y[512:768]
        new_state[2 * core:2 * core + 2] = np.asarray(r["ns"])
    return (y_prompt, y_sample, new_state)
```

```python
import contextlib
import numpy as np
import concourse.bass as bass
import concourse.mybir as mybir
from concourse.bass_utils import run_bass_kernel_spmd

F32 = mybir.dt.float32
BF16 = mybir.dt.bfloat16
AF = mybir.ActivationFunctionType
OP = mybir.AluOpType

D = 1024
NT = 768
EPS = 1e-6
NSLOT = 6
INFLIGHT = 3
SLOT_ELEMS = 4096
GELU = AF.Gelu_apprx_tanh


class Prog:
    def __init__(self, nc):
        self.nc = nc
        self.eng = {"pe": nc.tensor, "act": nc.scalar, "dve": nc.vector, "pool": nc.gpsimd, "sp": nc.sync}
        self.sems = {}
        self.cnt = {}
        self.known = {e: {} for e in self.eng}
        self.lw = {}
        self.rd = {}
        self.epoch = 0
        self.frontier = {}
        self.key_epoch = {}
        self.clock = {}
        self.nwait = 0
        self.nskip = 0
        for e in ("pe", "act", "dve", "pool"):
            self.newsem("c_" + e)

    def newsem(self, name):
        self.sems[name] = self.nc.alloc_semaphore(name)
        self.cnt[name] = 0
        return name

    def op(self, eng, emit, reads=(), writes=(), dsem=None):
        own = "c_" + eng
        waits = {}

        def need(sv, raw, war=False, waw=False):
            s, v = sv
            if s == own and dsem is None:
                if eng == "pe" or war or waw:
                    return
            waits[s] = max(waits.get(s, 0), v)

        for k in reads:
            if k in self.lw:
                need(self.lw[k], True)
            if k.startswith("ps"):
                for sv in self.rd.get(k, ()):
                    if sv[0] != own:
                        need(sv, False)
        for k in writes:
            if k in self.lw:
                need(self.lw[k], False, waw=not k.startswith(self.STRICT_WAW))
            for sv in self.rd.get(k, ()):
                need(sv, False, war=True)
            if self.frontier and not k.startswith(self.PERSISTENT) and self.key_epoch.get(k, -1) < self.epoch:
                for sv in self.frontier.items():
                    need(sv, False)
            self.key_epoch[k] = self.epoch
        e = self.eng[eng]
        kn = self.known[eng]
        for s, v in sorted(waits.items(), key=lambda sv: -sv[1]):
            if kn.get(s, 0) >= v:
                self.nskip += 1
                continue
            kn[s] = v
            e.wait_ge(self.sems[s], v)
            self.nwait += 1
            for s2, v2 in self.clock.get((s, v), {}).items():
                if kn.get(s2, 0) < v2:
                    kn[s2] = v2
        inst = emit(e)
        if dsem is not None:
            s, inc = dsem, 16
        else:
            s, inc = own, 1
        self.cnt[s] += inc
        val = self.cnt[s]
        inst.then_inc(self.sems[s], inc)
        self.clock[(s, val)] = dict(kn)
        for k in writes:
            self.lw[k] = (s, val)
            self.rd[k] = []
        for k in reads:
            self.rd.setdefault(k, []).append((s, val))

    STRICT_WAW = ("junk", "ps")

    PERSISTENT = ("ps", "xT", "hT", "ntmp", "nsq", "mod", "Amod", "slot", "st6", "mst", "dmaorder", "ident", "ones",
                  "vecAT", "vecBT", "scT", "lgB", "cdB", "c0B", "colsT", "Mmask", "qdec", "KD", "kd4", "kdo", "wsT")

    def soft_barrier(self):
        self.epoch += 1
        fr = dict(self.frontier)
        for s_, v in self.cnt.items():
            if s_.startswith("ring") or v == 0:
                continue
            fr[s_] = max(fr.get(s_, 0), v)
        self.frontier = fr

    def barrier(self, engines=("pe", "act", "dve", "sp"), skip_prefix="ring"):
        for eng in engines:
            e = self.eng[eng]
            for s, v in self.cnt.items():
                if s.startswith(skip_prefix) or s == "c_pool" or v == 0:
                    continue
                if s == "c_" + eng:
                    continue
                if self.known[eng].get(s, 0) >= v:
                    continue
                self.known[eng][s] = v
                e.wait_ge(self.sems[s], v)

    def final_wait(self, eng="sp"):
        e = self.eng[eng]
        for s, v in self.cnt.items():
            if v == 0:
                continue
            e.wait_ge(self.sems[s], v)


class _Stop(Exception):
    pass


def build_program():
    nc = bass.Bass("TRN2", target_bir_lowering=False)
    p = Prog(nc)
    try:
        _build(nc, p)
    except _Stop:
        p.final_wait("sp")
    return nc


def _build(nc, p):
    import os
    stop_at = os.environ.get("STOP_AT", "")

    def stage(tag):
        if stop_at == tag:
            raise _Stop()

    def din(name, shape):
        return nc.dram_tensor(name, list(shape), F32, kind="ExternalInput").ap()

    xin = din("xin", [NT, D])
    xoth = din("xoth", [NT, D])
    s0 = din("s0", [2, 4, 128, 256])
    vecA = din("vecA", [96, 128])
    vecB = din("vecB", [84, 128])
    cst = din("cst", [128, 6, 128])
    cols = din("cols", [128, 16])
    rde = din("rde", [128, 8])
    ident_d = din("ident_d", [128, 128])
    gws = din("gws", [4, 128, 128])
    W = {}
    for l in (0, 1):
        W[(l, "mod")] = din(f"l{l}_w_mod", [D, 6 * D])
        W[(l, "w1")] = din(f"l{l}_ffn_w1", [D, 4 * D])
        W[(l, "w2")] = din(f"l{l}_ffn_w2", [4 * D, D])
    W[(0, "in")] = din("l0_w_in", [D, 4096])
    W[(0, "out")] = din("l0_w_out", [1536, D])
    W[(1, "in")] = din("l1_w_in", [D, 3072])
    W[(1, "out")] = din("l1_w_out", [D, D])
    y_out = nc.dram_tensor("y", [NT, D], F32, kind="ExternalOutput").ap()
    ns_out = nc.dram_tensor("ns", [2, 2, 4, 128, 256], F32, kind="ExternalOutput").ap()

    top = contextlib.ExitStack()

    uniq = {"i": 0}

    def sb(es, name, shape, dt=F32):
        uniq["i"] += 1
        return es.enter_context(nc.sbuf_tensor(f"{name}_u{uniq['i']}", list(shape), dt))

    xT = sb(top, "xT", [128, 8, NT])
    hT = sb(top, "hT", [128, 8, NT], BF16)
    ring = [sb(top, f"ring{i}", [128, SLOT_ELEMS], BF16) for i in range(NSLOT)]
    ident = sb(top, "ident", [128, 128])
    ones_bf = sb(top, "ones_bf", [128, 128], BF16)
    ones_f = sb(top, "ones_f", [128, 128])
    vecAT = sb(top, "vecAT", [128, 96])
    vecBT = sb(top, "vecBT", [128, 84])
    scT = sb(top, "scT", [128, 8, 2], BF16)
    mod = [sb(top, f"mod{l}", [128, 48, 2]) for l in (0, 1)]
    Amod = sb(top, "Amod", [128, 4, 8, 2])
    lgB = sb(top, "lgB", [128, 8])
    cdB = sb(top, "cdB", [128, 8])
    c0B = sb(top, "c0B", [128, 8])
    colsT = sb(top, "colsT", [128, 16])
    Mmask = sb(top, "Mmask", [128, 4, 128])
    qdec = sb(top, "qdec", [128, 2, 4, 128])
    KD = sb(top, "KD", [128, 2, 512])
    kd4 = sb(top, "kd4", [128, 8])
    kdo = sb(top, "kdo", [128, 6, 8])
    wsT = sb(top, "wsT", [128, 4, 128], BF16)
    small = sb(top, "small", [128, 32])
    mst = sb(top, "mst", [2, 512])
    nsq = sb(top, "nsq", [128, 2, NT], BF16)
    ntmp = sb(top, "ntmp", [128, 2, NT])
    st6 = sb(top, "st6", [128, 16])
    PS = [top.enter_context(nc.psum_tensor(f"ps{i}", [128, 512], F32)) for i in range(8)]
    PK = [f"ps{i}" for i in range(8)]

    for i in range(NSLOT):
        p.newsem(f"ring{i}")
    for n in ("ld_a", "ld_b", "ld_c", "ld_d", "ld_e", "ld_f", "ld_g", "ld_h", "xs0", "xs1", "st_y0", "st_y1", "st_ns"):
        p.newsem(n)

    blocks = []

    def add_blocks(w, K, c0, c1, cw):
        wv = w.rearrange("(k p) n -> p k n", p=128)
        nk = K // 128
        for c in range(c0, c1, cw):
            ns = (nk * cw + SLOT_ELEMS - 1) // SLOT_ELEMS
            blocks.append((wv[:, :, c:c + cw], nk, cw, ns))

    add_blocks(W[(0, "mod")], D, 0, 2048, 512)
    add_blocks(W[(0, "in")], D, 0, 2048, 512)
    add_blocks(W[(0, "in")], D, 3584, 4096, 512)
    add_blocks(W[(0, "in")], D, 3072, 3584, 512)
    add_blocks(W[(0, "in")], D, 2048, 3072, 512)
    add_blocks(W[(0, "mod")], D, 2048, 3072, 512)
    for i_ in range(4):
        add_blocks(W[(0, "out")], 1536, 256 * i_, 256 * i_ + 256, 256)
        add_blocks(W[(0, "mod")], D, 3072 + 512 * i_, 3072 + 512 * i_ + 512, 512)
    add_blocks(W[(0, "mod")], D, 5120, 6144, 512)
    add_blocks(W[(1, "mod")], D, 0, 2048, 512)
    add_blocks(W[(0, "w1")], D, 0, 4096, 512)
    add_blocks(W[(0, "w2")], 4096, 0, 1024, 256)
    add_blocks(W[(1, "mod")], D, 2048, 3072, 512)
    add_blocks(W[(1, "in")], D, 0, 3072, 512)
    add_blocks(W[(1, "out")], D, 0, 512, 512)
    add_blocks(W[(1, "mod")], D, 3072, 3584, 512)
    add_blocks(W[(1, "out")], D, 512, 1024, 512)
    add_blocks(W[(1, "mod")], D, 3584, 5120, 512)
    add_blocks(W[(1, "mod")], D, 5120, 6144, 512)
    add_blocks(W[(1, "w1")], D, 0, 4096, 512)
    add_blocks(W[(1, "w2")], 4096, 0, 1024, 256)

    rstate = {"next_load": 0, "next_use": 0, "next_get": 0, "pos": 0, "loaded": []}

    def ring_issue_one():
        i = rstate["next_load"]
        if i >= len(blocks):
            return False
        src, nk, cw, ns = blocks[i]
        pos = rstate["pos"]
        if pos + ns > NSLOT:
            pos = 0
        occ = set()
        for (j, pj, nsj) in rstate["loaded"]:
            if j >= rstate["next_use"]:
                occ.update(range(pj, pj + nsj))
        if any(s in occ for s in range(pos, pos + ns)):
            return False
        rstate["pos"] = pos + ns
        rstate["next_load"] = i + 1
        rstate["loaded"].append((i, pos, ns))
        keys = [f"slot{s}" for s in range(pos, pos + ns)]
        order_rd = [f"dmaorder{i - INFLIGHT}"] if i >= INFLIGHT else []
        if i == 0:
            order_rd = [f"xst{t}" for t in range(6)]
        if ns == 1:
            dst = ring[pos][:, 0:nk * cw].rearrange("p (k n) -> p k n", k=nk)
            p.op("pool", lambda e: e.dma_start(out=dst, in_=src), reads=order_rd, writes=keys + [f"dmaorder{i}"], dsem=f"ring{pos}")
        else:
            hk = nk // 2
            for hh in range(2):
                dst = ring[pos + hh][:, 0:hk * cw].rearrange("p (k n) -> p k n", k=hk)
                srch = src[:, hh * hk:(hh + 1) * hk, :]
                p.op("pool", (lambda e, dst=dst, srch=srch: e.dma_start(out=dst, in_=srch)),
                     reads=order_rd, writes=[keys[hh]] + ([f"dmaorder{i}"] if hh == 1 else []), dsem=f"ring{pos + hh}")
        return True

    def ring_fill():
        while ring_issue_one():
            pass

    class Blk:
        pass

    def ring_get():
        i = rstate["next_get"]
        rstate["next_get"] += 1
        ring_fill()
        ent = [e for e in rstate["loaded"] if e[0] == i]
        assert ent, f"block {i} not loaded"
        _, pos, ns = ent[0]
        src, nk, cw, _ = blocks[i]
        b = Blk()
        b.keys = [f"slot{s}" for s in range(pos, pos + ns)]
        if ns == 1:
            v = ring[pos][:, 0:nk * cw].rearrange("p (k n) -> p k n", k=nk)
            b.w = lambda k, c0, c1: v[:, k, c0:c1]
        else:
            hk = nk // 2
            vs = [ring[pos + hh][:, 0:hk * cw].rearrange("p (k n) -> p k n", k=hk) for hh in range(2)]
            b.w = lambda k, c0, c1: vs[k // hk][:, k % hk, c0:c1]
        return b

    def ring_done(n=1):
        rstate["next_use"] += n
        assert rstate["next_use"] <= rstate["next_get"]
        ring_fill()

    def mm_group(out_ap, out_key, pairs, reads):
        n = len(pairs)

        def emit(e):
            inst = None
            for i, (l, r) in enumerate(pairs):
                inst = e.matmul(out_ap, l, r, start=(i == 0), stop=(i == n - 1))
            return inst
        p.op("pe", emit, reads=reads, writes=[out_key])

    def transposes(out_key, items, reads):
        def emit(e):
            inst = None
            for (o, i_, n) in items:
                inst = e.transpose(o, i_, ident[0:n, 0:n])
            return inst
        p.op("pe", emit, reads=list(reads) + ["ident"], writes=[out_key])

    def act(out, in_, func, reads, writes, bias=None, scale=None, accum_out=None, eng="act"):
        kw = {}
        if bias is not None:
            kw["bias"] = bias
        if scale is not None:
            kw["scale"] = scale
        if accum_out is not None:
            kw["accum_out"] = accum_out
        p.op("act", lambda e: e.activation(out, in_, func, **kw), reads=reads, writes=writes)

    def tt(out, a, b, op, reads, writes, eng="dve"):
        p.op(eng, lambda e: e.tensor_tensor(out, a, b, op), reads=reads, writes=writes)

    def ts(out, a, s1, s2, op0, op1, reads, writes, eng="dve"):
        if op1 is None:
            p.op(eng, lambda e: e.tensor_scalar(out, a, s1, None, op0), reads=reads, writes=writes)
        else:
            p.op(eng, lambda e: e.tensor_scalar(out, a, s1, s2, op0, op1), reads=reads, writes=writes)

    def stt(out, a, s, b, op0, op1, reads, writes):
        p.op("dve", lambda e: e.scalar_tensor_tensor(out, a, s, b, op0, op1), reads=reads, writes=writes)

    def cp(out, in_, reads, writes, eng="dve"):
        if eng == "act":
            p.op("act", lambda e: e.activation(out, in_, AF.Copy), reads=reads, writes=writes)
        else:
            p.op(eng, lambda e: e.tensor_copy(out, in_), reads=reads, writes=writes)

    def dma(q, out, in_, reads, writes, sem):
        p.op(q, lambda e: e.dma_start(out=out, in_=in_), reads=reads, writes=writes, dsem=sem)

    dma("sp", ident[:, :], ident_d[:, :], [], ["ident"], "ld_a")
    setup = contextlib.ExitStack()
    xst = sb(setup, "xst", [128, 6, D])
    for i in range(6):
        p.newsem(f"xl{i}")
    vA = sb(setup, "vA", [96, 128])
    vB = sb(setup, "vB", [84, 128])
    cstT = sb(setup, "cstT", [128, 6, 128])
    rdeT = sb(setup, "rdeT", [128, 8])
    gwt = sb(setup, "gwt", [128, 4, 128])
    tmpM = sb(setup, "tmpM", [128, 2, 128])
    dma("sp", vA[:, :], vecA[:, :], [], ["vA"], "ld_b")
    dma("sp", vB[:, :], vecB[:, :], [], ["vB"], "ld_c")
    dma("sp", cstT[:, :, :], cst[:, :, :], [], ["cstT"], "ld_d")
    dma("sp", rdeT[:, :], rde[:, :], [], ["rdeT"], "ld_e")
    dma("sp", colsT[:, :], cols[:, :], [], ["colsT"], "ld_f")
    dma("sp", gwt[:, :, :], gws.rearrange("g p q -> p g q"), [], ["gwt"], "ld_g")
    for t in range(6):
        dma("sp", xst[:, t, :], xin[t * 128:(t + 1) * 128, :], [], [f"xst{t}"], f"xl{t}")
    ring_fill()
    p.op("dve", lambda e: e.memset(ones_bf[:, :], 1.0), writes=["ones_bf"])
    p.op("dve", lambda e: e.memset(ones_f[:, :], 1.0), writes=["ones_f"])
    transposes(PK[4], [(PS[4][:, 0:96], vA[:, :], 96), (PS[4][:, 128:212], vB[:, :], 84)], ["vA", "vB"])
    cp(vecAT[:, :], PS[4][:, 0:96], [PK[4]], ["vecAT"])
    cp(vecBT[:, :], PS[4][:, 128:212], [PK[4]], ["vecBT"])
    for c in range(2):
        act(scT[:, :, c], vecBT[:, 40 + 8 * c:48 + 8 * c], AF.Silu, ["vecBT"], ["scT"])
    transposes(PK[5], [(PS[5][:, g * 128:(g + 1) * 128], gwt[:, g, :], 128) for g in range(4)], ["gwt"])
    cp(wsT[:, :, :], PS[5][:, :].rearrange("p (g q) -> p g q", g=4), [PK[5]], ["wsT"])
    act(lgB[:, :], rdeT[:, :], AF.Exp, ["rdeT"], ["lgB"], scale=-float(np.log(2.0)))
    act(lgB[:, :], lgB[:, :], AF.Ln, ["lgB"], ["lgB"], scale=-1.0, bias=1.0)
    act(cdB[:, :], lgB[:, :], AF.Exp, ["lgB"], ["cdB"], scale=128.0)
    for d in range(2):
        act(c0B[:, 4 * d:4 * d + 4], lgB[:, 4 * d:4 * d + 4], AF.Exp, ["lgB", "colsT"], ["c0B"], scale=colsT[:, 2 + d:3 + d])
        act(kd4[:, 4 * d:4 * d + 4], lgB[:, 4 * d:4 * d + 4], AF.Exp, ["lgB", "colsT"], ["kd4"], scale=colsT[:, d:d + 1])
        for t in range(6):
            act(kdo[:, t, 4 * d:4 * d + 4], lgB[:, 4 * d:4 * d + 4], AF.Exp, ["lgB", "colsT"], ["kdo"],
                scale=colsT[:, 4 + 2 * t + d:5 + 2 * t + d])
    sdk = float(128 ** -0.5)
    ts(kdo[:, :, :], kdo[:, :, :], sdk, None, OP.mult, None, ["kdo"], ["kdo"])
    for d in range(2):
        for h in range(4):
            ts(KD[:, d, h * 128:(h + 1) * 128], ones_f[:, :], kd4[:, 4 * d + h:4 * d + h + 1], sdk, OP.mult, OP.mult,
               ["ones_f", "kd4"], ["KD"])
            act(qdec[:, d, h, :], cstT[:, 4 + d, :], AF.Exp, ["cstT", "lgB"], ["qdec"], scale=lgB[:, 4 * d + h:4 * d + h + 1])
    for h in range(4):
        act(tmpM[:, 0, :], cstT[:, 0, :], AF.Exp, ["cstT", "lgB"], ["tmpM0"], scale=lgB[:, h:h + 1])
        act(tmpM[:, 1, :], cstT[:, 1, :], AF.Exp, ["cstT", "lgB"], ["tmpM1"], scale=lgB[:, 4 + h:5 + h])
        tt(tmpM[:, 0, :], tmpM[:, 0, :], cstT[:, 2, :], OP.mult, ["tmpM0", "cstT"], ["tmpM0"])
        tt(tmpM[:, 1, :], tmpM[:, 1, :], cstT[:, 3, :], OP.mult, ["tmpM1", "cstT"], ["tmpM1"])
        tt(Mmask[:, h, :], tmpM[:, 0, :], tmpM[:, 1, :], OP.add, ["tmpM0", "tmpM1"], ["Mmask"])
    stage("setup")

    def adaln_gen(l, which):
        for half in range(2):
            b = ring_get()
            mm_group(PS[6][0:2, :], PK[6], [(scT[:, k, :], b.w(k, 0, 512)) for k in range(8)], ["scT"] + b.keys)
            ring_done()
            cp(mst[0:2, :], PS[6][0:2, :], [PK[6]], ["mst"], eng="act")
            transposes(PK[7], [(PS[7][:, 8 * half + 2 * j:8 * half + 2 * j + 2], mst[0:2, j * 128:(j + 1) * 128], 2) for j in range(4)],
                       ["mst"])
            if half == 0:
                yield
        for c in range(2):
            tt(mod[l][:, which * 8:(which + 1) * 8, c], PS[7][:, 0:16].rearrange("p (j c) -> p j c", c=2)[:, :, c],
               vecAT[:, l * 48 + which * 8:l * 48 + which * 8 + 8], OP.add, [PK[7], "vecAT"], [f"mod{l}_{which}"])
        if which in (1, 4):
            ni = 0 if which == 1 else 1
            for c in range(2):
                stt(Amod[:, l * 2 + ni, :, c], mod[l][:, which * 8:(which + 1) * 8, c], 1.0,
                    vecBT[:, (l * 2 + ni) * 8:(l * 2 + ni) * 8 + 8], OP.add, OP.mult,
                    [f"mod{l}_{which}", "vecBT"], [f"Amod{l}{ni}"])
        yield

    def adaln_block(l, which, es=None):
        for _ in adaln_gen(l, which):
            pass

    def chain(*gens):
        for g in gens:
            yield from g

    def zip_emit(main, filler):
        m_done = f_done = False
        while not (m_done and f_done):
            if not m_done:
                try:
                    next(main)
                except StopIteration:
                    m_done = True
            if not f_done:
                try:
                    next(filler)
                except StopIteration:
                    f_done = True

    GRPS = [(0, 512, 0), (512, 768, 1)]

    stat = {"pending": [], "n": 0, "tok": False}
    ACC = PS[5][:, 384:390]

    def stats_chunk(k):
        r = stat["n"] % 2
        stat["n"] += 1
        act(nsq[:, r, :], xT[:, k, :], AF.Square, ["xT"], [f"nsq{r}"])
        first, last = (k == 0), (k == 7)
        tokmode = stat["tok"]

        def pe_part():
            if tokmode:
                def emit(e):
                    inst = None
                    for t in range(6):
                        inst = e.matmul(PS[5][:, 384 + t:385 + t], nsq[:, r, t * 128:(t + 1) * 128], ones_bf[:, 0:1],
                                        start=(first and t == 0), stop=last, skip_group_check=True)
                    return inst
                p.op("pe", emit, reads=["ones_bf", f"nsq{r}"], writes=[PK[5]])
            else:
                for gi, (a, b_, c) in enumerate(GRPS):
                    def emit(e, gi=gi, a=a, b_=b_):
                        return e.matmul(PS[4 + gi][:, 0:b_ - a], ones_bf[:, :], nsq[:, r, a:b_], start=first, stop=last)
                    p.op("pe", emit, reads=["ones_bf", f"nsq{r}"], writes=[PK[4 + gi]])
        stat["pending"].append(pe_part)

    def stats_flush(keep=0):
        while len(stat["pending"]) > keep:
            stat["pending"].pop(0)()

    def stats_finish(final=False):
        stats_flush(0)
        if final:
            act(st6[:, 0:6], ACC, AF.Ln, [PK[5]], ["st6"], scale=1.0 / D, bias=EPS)
            act(st6[:, 8:14], st6[:, 0:6], AF.Exp, ["st6"], ["st6b"], scale=-0.5)
            return
        for gi, (a, b_, c) in enumerate(GRPS):
            ps = PS[4 + gi][:, 0:b_ - a]
            act(ps, ps, AF.Ln, [PK[4 + gi]], [PK[4 + gi]], scale=1.0 / D, bias=EPS)
            act(ps, ps, AF.Exp, [PK[4 + gi]], [PK[4 + gi]], scale=-0.5)

    def modulate_gen(l, ni):
        for k in range(8):
            for gi, (a, b_, c) in enumerate(GRPS):
                tt(ntmp[:, k % 2, a:b_], xT[:, k, a:b_], PS[4 + gi][:, 0:b_ - a], OP.mult, ["xT", PK[4 + gi]], [f"ntmp{k % 2}{gi}"])
            for gi, (a, b_, c) in enumerate(GRPS):
                rd = [f"ntmp{k % 2}{gi}", f"Amod{l}{ni}", f"mod{l}_{ni * 3}"]
                if gi == 0:
                    act(hT[:, k, a:b_], ntmp[:, k % 2, a:b_], AF.Identity, rd, ["hT0"],
                        scale=Amod[:, l * 2 + ni, k, c:c + 1], bias=mod[l][:, ni * 24 + k, c:c + 1])
                else:
                    ts(hT[:, k, a:b_], ntmp[:, k % 2, a:b_], Amod[:, l * 2 + ni, k, c:c + 1], mod[l][:, ni * 24 + k, c:c + 1],
                       OP.mult, OP.add, rd, ["hT1"])
            yield

    def modulate(l, ni, filler=None):
        zip_emit(modulate_gen(l, ni), filler if filler is not None else iter(()))

    def xload():
        for t in range(6):
            for half in range(2):
                bk = 4 + (2 * t + half) % 2
                transposes(PK[bk], [(PS[bk][:, j * 128:(j + 1) * 128], xst[:, t, (half * 4 + j) * 128:(half * 4 + j + 1) * 128], 128)
                                    for j in range(4)], [f"xst{t}"])
                cp(xT[:, half * 4:half * 4 + 4, t * 128:(t + 1) * 128], PS[bk][:, :].rearrange("p (j n) -> p j n", j=4),
                   [PK[bk]], ["xT"], eng=("act" if half == 0 else "dve"))

    L0 = contextlib.ExitStack()
    xload()
    stage("xload")
    for k in range(8):
        stats_chunk(k)
        stats_flush(1)
    stats_flush(0)
    adaln_block(0, 0, L0)
    adaln_block(0, 1, L0)
    stage("adaln0")
    stats_finish()
    p.soft_barrier()
    setup.close()
    modulate(0, 0)
    stage("norm0")

    P1 = contextlib.ExitStack()
    onorm = sb(P1, "onorm", [128, 6, 1024], BF16)

    PA = contextlib.ExitStack()
    q_tok = sb(PA, "q_tok", [128, 512])
    k_tok = sb(PA, "k_tok", [128, 512])
    ksc2 = sb(PA, "ksc", [128, 2, 2, 2, 512], BF16)
    v_tok2 = sb(PA, "v_tok", [128, 2, 2, 1024], BF16)
    qT2 = sb(PA, "qT", [128, 2, 4, 256], BF16)
    kT2 = sb(PA, "kT", [128, 2, 4, 256], BF16)
    qdT2 = sb(PA, "qdT", [128, 2, 2, 4, 256], BF16)
    KSC = [ksc2[:, i] for i in range(2)]
    VT = [v_tok2[:, i] for i in range(2)]
    QT = [qT2[:, i] for i in range(2)]
    KT = [kT2[:, i] for i in range(2)]
    QDT = [qdT2[:, i] for i in range(2)]
    PT = sb(PA, "PT", [128, 2, 4, 128], BF16)
    Sfp = sb(PA, "Sfp", [128, 8, 256])
    Sin_bf = sb(PA, "Sin_bf", [128, 8, 256], BF16)
    S1_bf = sb(PA, "S1_bf", [128, 8, 256], BF16)
    junk = sb(PA, "junk", [128, 2, 256], BF16)
    ss4 = sb(PA, "ss4", [128, 2, 4])
    xo = sb(PA, "xo", [128, 2, D])
    dgo = sb(PA, "dgo", [128, 2, 128])
    junk2 = sb(PA, "junk2", [128, D], BF16)
    hoT = sb(PA, "hoT", [128, 2, 8, 128], BF16)
    osm = sb(PA, "osm", [128, 2, 4])
    kso = sb(PA, "kso", [128, 2, 512], BF16)
    vo = sb(PA, "vo", [128, 1024], BF16)
    SinF = sb(PA, "SinF", [128, 8, 256])

    wq = ring_get()
    wk = ring_get()
    wv0 = ring_get()
    wv1 = ring_get()
    psrot = {"i": 0}

    def nextbank():
        b = psrot["i"] % 4
        psrot["i"] += 1
        return b

    def inproj(lhs_fn, lhs_keys, blk, evac):
        bk = nextbank()
        mm_group(PS[bk][:, :], PK[bk], [(lhs_fn(k), blk.w(k, 0, 512)) for k in range(8)], lhs_keys + blk.keys)
        evac(bk)

    def passA_tile(t, tt_i, sq):
        ksc, v_tok, qT, kT, qdT = KSC[sq], VT[sq], QT[sq], KT[sq], QDT[sq]
        cs = slice(t * 128, (t + 1) * 128)
        lhs = lambda k: hT[:, k, cs]
        def ev_q(bk):
            cp(q_tok[:, :], PS[bk][:, :], [PK[bk]], ["q_tok"], eng="act")
        inproj(lhs, ["hT0", "hT1"], wq, ev_q)
        yield
        def ev_k(bk):
            cp(k_tok[:, :], PS[bk][:, :], [PK[bk]], ["k_tok"], eng="act")
            for d in range(2):
                tt(ksc[:, tt_i, d, :], PS[bk][:, :], KD[:, d, :], OP.mult, [PK[bk], "KD"], [f"ksc{sq}{tt_i}"])
        inproj(lhs, ["hT0", "hT1"], wk, ev_k)
        yield
        for hv, wv_ in enumerate((wv0, wv1)):
            def ev_v(bk, hv=hv):
                cp(v_tok[:, tt_i, hv * 512:(hv + 1) * 512], PS[bk][:, :], [PK[bk]], [f"v{sq}{tt_i}h{hv}"],
                   eng=("act" if hv == 0 else "dve"))
            inproj(lhs, ["hT0", "hT1"], wv_, ev_v)
            yield
        bq = nextbank()
        transposes(PK[bq], [(PS[bq][:, h * 128:(h + 1) * 128], q_tok[:, h * 128:(h + 1) * 128], 128) for h in range(4)], ["q_tok"])
        ls = slice(tt_i * 128, (tt_i + 1) * 128)
        psqv = PS[bq][:, :].rearrange("p (h n) -> p h n", h=4)
        cp(qT[:, :, ls], psqv, [PK[bq]], [f"qT{sq}{tt_i}"], eng="act")
        for d in range(2):
            tt(qdT[:, d, :, ls], psqv, qdec[:, d, :, :], OP.mult, [PK[bq], "qdec"], [f"qdT{sq}{tt_i}"])
        bkk = nextbank()
        transposes(PK[bkk], [(PS[bkk][:, h * 128:(h + 1) * 128], k_tok[:, h * 128:(h + 1) * 128], 128) for h in range(4)], ["k_tok"])
        cp(kT[:, :, ls], PS[bkk][:, :].rearrange("p (h n) -> p h n", h=4), [PK[bkk]], [f"kT{sq}{tt_i}"], eng="act")
        yield

    Sfp4 = Sfp[:, :, :].rearrange("p (d h) e -> p d h e", d=2)
    S1_bf4 = S1_bf[:, :, :].rearrange("p (d h) e -> p d h e", d=2)

    def retention(seq, is_sample):
        sq = seq % 2
        ksc, v_tok, qT, kT, qdT = KSC[sq], VT[sq], QT[sq], KT[sq], QDT[sq]
        for c in range(2):
            ls = slice(c * 128, (c + 1) * 128)
            bk = 4 + c
            def emit(e, ls=ls, bk=bk):
                inst = None
                for h in range(4):
                    inst = e.matmul(PS[bk][:, h * 128:(h + 1) * 128], kT[:, h, ls], qT[:, h, ls], start=True, stop=True)
                return inst
            p.op("pe", emit, reads=[f"kT{sq}{c}", f"qT{sq}{c}"], writes=[PK[bk]])
            tt(PT[:, c, :, :], PS[bk][:, :].rearrange("p (h n) -> p h n", h=4), Mmask[:, :, :], OP.mult,
               [PK[bk], "Mmask"], [f"PT{c}"])
            yield
        for h in range(4):
            bk = 6 + (h % 2)
            def emit(e, h=h, bk=bk):
                e.matmul(PS[bk][:, 0:256], ksc[:, 0, 0, h * 128:(h + 1) * 128], v_tok[:, 0, h * 256:(h + 1) * 256], start=True, stop=True)
                return e.matmul(PS[bk][:, 256:512], ksc[:, 1, 1, h * 128:(h + 1) * 128], v_tok[:, 1, h * 256:(h + 1) * 256], start=True, stop=True)
            p.op("pe", emit, reads=[f"ksc{sq}0", f"ksc{sq}1", f"v{sq}0h0", f"v{sq}0h1", f"v{sq}1h0", f"v{sq}1h1"], writes=[PK[bk]])
            if is_sample:
                for d in range(2):
                    stt(S1_bf[:, 4 * d + h, :], SinF[:, 4 * d + h, :], cdB[:, 4 * d + h:4 * d + h + 1],
                        PS[bk][:, d * 256:(d + 1) * 256], OP.mult, OP.add, ["SinF", "cdB", PK[bk]], ["S1_bf"])
            else:
                cp(Sfp4[:, :, h, :], PS[bk][:, :].rearrange("p (d e) -> p d e", d=2), [PK[bk]], ["Sfp"], eng="act")
                cp(S1_bf4[:, :, h, :], Sfp4[:, :, h, :], ["Sfp"], ["S1_bf"], eng="dve")
            yield
        for c in range(2):
            ls = slice(c * 128, (c + 1) * 128)
            t = 2 * seq + c
            for hp in range(2):
                bk = 4 + hp if c == 0 else 6 + hp
                def emit(e, c=c, hp=hp, bk=bk, ls=ls):
                    inst = None
                    for hh in range(2):
                        h = hp * 2 + hh
                        o_ap = PS[bk][:, hh * 256:(hh + 1) * 256]
                        terms = [(PT[:, c, h, :], v_tok[:, c, h * 256:(h + 1) * 256])]
                        if c == 0:
                            if is_sample:
                                terms.append((qdT[:, 0, h, ls], Sin_bf[:, h, :]))
                            terms.append((qdT[:, 1, h, ls], S1_bf[:, 4 + h, :]))
                        else:
                            terms.append((qdT[:, 0, h, ls], S1_bf[:, h, :]))
                            if is_sample:
                                terms.append((qdT[:, 1, h, ls], Sin_bf[:, 4 + h, :]))
                        for i, (l_, r_) in enumerate(terms):
                            inst = e.matmul(o_ap, l_, r_, start=(i == 0), stop=(i == len(terms) - 1))
                    return inst
                p.op("pe", emit, reads=[f"PT{c}", f"v{sq}{c}h0", f"v{sq}{c}h1", f"qdT{sq}{c}", "Sin_bf", "S1_bf"], writes=[PK[bk]])
                for hh in range(2):
                    h = hp * 2 + hh
                    act(junk[:, hh, :], PS[bk][:, hh * 256:(hh + 1) * 256], AF.Square, [PK[bk]], [f"junk{hh}", f"ss4{c}"],
                        accum_out=ss4[:, c, h:h + 1])
                yield
            act(ss4[:, c, :], ss4[:, c, :], AF.Ln, [f"ss4{c}"], [f"ss4{c}"], scale=1.0 / 256.0, bias=EPS)
            act(ss4[:, c, :], ss4[:, c, :], AF.Exp, [f"ss4{c}"], [f"ss4{c}"], scale=-0.5)
            for hp in range(2):
                bk = 4 + hp if c == 0 else 6 + hp
                for hh in range(2):
                    h = hp * 2 + hh
                    ts(onorm[:, t, h * 256:(h + 1) * 256], PS[bk][:, hh * 256:(hh + 1) * 256], ss4[:, c, h:h + 1], 0.5,
                       OP.mult, OP.mult, [PK[bk], f"ss4{c}"], [f"onorm{t}"])
            yield
        if not is_sample:
            for h in range(4):
                bk = 4 + (h % 2)
                def emit(e, h=h, bk=bk):
                    e.matmul(PS[bk][:, 0:256], ksc[:, 1, 0, h * 128:(h + 1) * 128], v_tok[:, 1, h * 256:(h + 1) * 256], start=True, stop=True)
                    return e.matmul(PS[bk][:, 256:512], ksc[:, 0, 1, h * 128:(h + 1) * 128], v_tok[:, 0, h * 256:(h + 1) * 256], start=True, stop=True)
                p.op("pe", emit, reads=[f"ksc{sq}0", f"ksc{sq}1", f"v{sq}0h0", f"v{sq}0h1", f"v{sq}1h0", f"v{sq}1h1"], writes=[PK[bk]])
                for d in range(2):
                    stt(Sfp[:, 4 * d + h, :], Sfp[:, 4 * d + h, :], cdB[:, 4 * d + h:4 * d + h + 1],
                        PS[bk][:, d * 256:(d + 1) * 256], OP.mult, OP.add, ["Sfp", "cdB", PK[bk]], ["Sfp"])
                yield
            for d in range(2):
                dma("sp", ns_out[seq, d].rearrange("h d e -> d h e"), Sfp[:, 4 * d:4 * d + 4, :], ["Sfp"], [], "st_ns")

    def others_A(t):
        r = t % 2
        dma("sp", xo[:, r, :], xoth[t * 128:(t + 1) * 128, :], [], [f"xo{r}"], f"xs{r}")
        act(junk2[:, :], xo[:, r, :], AF.Square, [f"xo{r}"], ["junk2", f"osm{r}"], accum_out=osm[:, r, 0:1])
        act(osm[:, r, 1:2], osm[:, r, 0:1], AF.Ln, [f"osm{r}"], [f"osmb{r}"], scale=1.0 / D, bias=EPS)
        act(osm[:, r, 2:3], osm[:, r, 1:2], AF.Exp, [f"osmb{r}"], [f"osmc{r}"], scale=-0.5)
        ts(dgo[:, r, :], ident[:, :], osm[:, r, 2:3], None, OP.mult, None, ["ident", f"osmc{r}"], [f"dgo{r}"])

    def others_B1(t):
        r = t % 2
        for half in range(2):
            bk = nextbank()
            def emit(e, half=half, bk=bk, r=r):
                inst = None
                for j in range(4):
                    k = half * 4 + j
                    inst = e.matmul(PS[bk][:, j * 128:(j + 1) * 128], xo[:, r, k * 128:(k + 1) * 128], dgo[:, r, :],
                                    start=True, stop=True)
                return inst
            p.op("pe", emit, reads=[f"xo{r}", f"dgo{r}"], writes=[PK[bk]])
            for j in range(4):
                k = half * 4 + j
                if half == 0:
                    ts(hoT[:, r, k, :], PS[bk][:, j * 128:(j + 1) * 128], Amod[:, 0, k, 1:2], mod[0][:, k, 1:2], OP.mult, OP.add,
                       [PK[bk], "Amod00", "mod0_0"], [f"hoT{r}a"])
                else:
                    act(hoT[:, r, k, :], PS[bk][:, j * 128:(j + 1) * 128], AF.Identity, [PK[bk], "Amod00", "mod0_0"], [f"hoT{r}b"],
                        scale=Amod[:, 0, k, 1:2], bias=mod[0][:, k, 1:2])

    def others_B2(t):
        r = t % 2
        lhs = lambda k: hoT[:, r, k, :]
        def ev_ko(bk, t=t):
            for h in range(4):
                ts(kso[:, 0, h * 128:(h + 1) * 128], PS[bk][:, h * 128:(h + 1) * 128], kdo[:, t, h:h + 1], None,
                   OP.mult, None, [PK[bk], "kdo"], ["kso0"])
            for h in range(4):
                act(kso[:, 1, h * 128:(h + 1) * 128], PS[bk][:, h * 128:(h + 1) * 128], AF.Copy, [PK[bk], "kdo"], ["kso1"],
                    scale=kdo[:, t, 4 + h:5 + h])
        inproj(lhs, [f"hoT{r}a", f"hoT{r}b"], wk, ev_ko)
        for hv, wv_ in enumerate((wv0, wv1)):
            def ev_vo(bk, hv=hv):
                cp(vo[:, hv * 512:(hv + 1) * 512], PS[bk][:, :], [PK[bk]], [f"vo{hv}"], eng=("act" if hv == 0 else "dve"))
            inproj(lhs, [f"hoT{r}a", f"hoT{r}b"], wv_, ev_vo)

    def others_B3(t):
        def emit(e, t=t):
            inst = None
            for h in range(4):
                bk = 4 + h
                e.matmul(PS[bk][:, 0:256], kso[:, 0, h * 128:(h + 1) * 128], vo[:, h * 256:(h + 1) * 256],
                         start=(t == 0), stop=(t == 5), skip_group_check=True)
                inst = e.matmul(PS[bk][:, 256:512], kso[:, 1, h * 128:(h + 1) * 128], vo[:, h * 256:(h + 1) * 256],
                                start=False, stop=(t == 5), skip_group_check=True)
            return inst
        p.op("pe", emit, reads=["kso0", "kso1", "vo0", "vo1"], writes=[PK[4], PK[5], PK[6], PK[7]])
        if t == 5:
            SinF4 = SinF[:, :, :].rearrange("p (d h) e -> p d h e", d=2)
            for h in range(4):
                bk = 4 + h
                tt(SinF4[:, :, h, :], SinF4[:, :, h, :], PS[bk][:, :].rearrange("p (d e) -> p d e", d=2), OP.add,
                   ["SinF", PK[bk]], ["SinF"])

    def interleave(main, fillers):
        fillers = list(fillers)
        main_done = False
        while not main_done or fillers:
            if not main_done:
                try:
                    next(main)
                except StopIteration:
                    main_done = True
            if fillers:
                try:
                    next(fillers[0])
                except StopIteration:
                    fillers.pop(0)

    dma("sp", SinF[:, :, :], s0.rearrange("a h d e -> d (a h) e"), [], ["SinF"], "ld_h")
    for j in range(8):
        ts(SinF[:, j, :], SinF[:, j, :], c0B[:, j:j + 1], None, OP.mult, None, ["SinF", "c0B"], ["SinF"])

    def seq_inproj(seq):
        for tt_i in range(2):
            yield from passA_tile(2 * seq + tt_i, tt_i, seq % 2)

    for _ in seq_inproj(0):
        pass
    zip_emit(retention(0, False), seq_inproj(1))
    stage("ret0")
    zip_emit(retention(1, False), seq_inproj(2))
    stage("ret1")
    others_A(0)
    others_A(1)
    others_B1(0)
    for t in range(6):
        if t + 2 < 6:
            others_A(t + 2)
        others_B2(t)
        if t + 1 < 6:
            others_B1(t + 1)
        others_B3(t)
    cp(Sin_bf[:, :, :], SinF[:, :, :], ["SinF"], ["Sin_bf"], eng="dve")
    stage("others")
    for _ in retention(2, True):
        pass
    ring_done(4)
    p.soft_barrier()
    PA.close()
    stage("passA")
    P1b = contextlib.ExitStack()
    mixT = sb(P1b, "mixT", [128, 12, NT], BF16)

    PB = contextlib.ExitStack()
    gtmp = sb(PB, "gtmp", [128, 2, 512])
    mix_tok = sb(PB, "mix_tok", [128, 2, 1536])
    u_tok = sb(PB, "u_tok", [128, 2, 512])
    zg = sb(PB, "zg", [128, 2, 512])
    zn = sb(PB, "zn", [128, 2, 512], BF16)
    bst = sb(PB, "bst", [128, 2, 16])
    mhalf = sb(PB, "mhalf", [128, 2])
    p.op("dve", lambda e: e.memset(mhalf[:, :], -0.5), writes=["mhalf"])
    wz = ring_get()
    wu = ring_get()
    wg0 = ring_get()
    wg1 = ring_get()
    def passB_zu(t):
        r = t % 2
        cs = slice(t * 128, (t + 1) * 128)
        lhs = lambda k, cs=cs: hT[:, k, cs]
        def ev_z(bk, r=r):
            act(zg[:, r, :], PS[bk][:, :], GELU, [PK[bk]], [f"zg{r}"])
            p.op("dve", lambda e: e.bn_stats(bst[:, r, 0:6], zg[:, r, :]), reads=[f"zg{r}"], writes=[f"bst{r}"])
            p.op("dve", lambda e: e.bn_aggr(bst[:, r, 8:10], bst[:, r, 0:6]), reads=[f"bst{r}"], writes=[f"bstb{r}"])
            ts(bst[:, r, 10:11], bst[:, r, 9:10], EPS, None, OP.add, None, [f"bstb{r}"], [f"bstc{r}"])
            p.op("pool", lambda e: e.tensor_tensor(bst[:, r, 11:12], bst[:, r, 10:11], mhalf[:, 0:1], OP.pow),
                 reads=[f"bstc{r}", "mhalf"], writes=[f"bstd{r}"])
            ts(zn[:, r, :], zg[:, r, :], bst[:, r, 8:9], bst[:, r, 11:12], OP.subtract, OP.mult,
               [f"zg{r}", f"bstb{r}", f"bstd{r}"], [f"zn{r}"])
        inproj(lhs, ["hT0", "hT1"], wz, ev_z)
        def ev_u(bk, r=r):
            act(u_tok[:, r, :], PS[bk][:, :], GELU, [PK[bk]], [f"u_tok{r}"])
        inproj(lhs, ["hT0", "hT1"], wu, ev_u)

    def passB_sv(t):
        r = t % 2
        def emit(e, r=r):
            inst = None
            for g in range(4):
                inst = e.matmul(PS[4][:, g * 128:(g + 1) * 128], wsT[:, g, :], zn[:, r, g * 128:(g + 1) * 128], start=True, stop=True)
            return inst
        p.op("pe", emit, reads=["wsT", f"zn{r}"], writes=[PK[4]])
        for g in range(4):
            stt(mix_tok[:, r, 1024 + g * 128:1024 + (g + 1) * 128], PS[4][:, g * 128:(g + 1) * 128], vecBT[:, 80 + g:81 + g],
                u_tok[:, r, g * 128:(g + 1) * 128], OP.add, OP.mult, [PK[4], "vecBT", f"u_tok{r}"], [f"mix_gm{r}"])

    def passB_g(t):
        r = t % 2
        cs = slice(t * 128, (t + 1) * 128)
        lhs = lambda k, cs=cs: hT[:, k, cs]
        for gi, wg in enumerate((wg0, wg1)):
            def ev_g(bk, gi=gi, t=t, r=r):
                act(gtmp[:, gi, :], PS[bk][:, :], AF.Tanh, [PK[bk]], [f"gtmp{gi}"], scale=0.5)
                stt(gtmp[:, gi, :], gtmp[:, gi, :], 1.0, PS[bk][:, :], OP.add, OP.mult, [f"gtmp{gi}", PK[bk]], [f"gtmp{gi}"])
                tt(mix_tok[:, r, gi * 512:(gi + 1) * 512], gtmp[:, gi, :], onorm[:, t, gi * 512:(gi + 1) * 512], OP.mult,
                   [f"gtmp{gi}", f"onorm{t}"], [f"mix_g{gi}{r}"])
            inproj(lhs, ["hT0", "hT1"], wg, ev_g)

    def passB_T(t):
        r = t % 2
        cs = slice(t * 128, (t + 1) * 128)
        for j3 in range(3):
            bk = 5 + j3
            rk = [f"mix_g0{r}", f"mix_g1{r}", f"mix_gm{r}"][j3] if j3 < 2 else f"mix_gm{r}"
            transposes(PK[bk], [(PS[bk][:, j * 128:(j + 1) * 128], mix_tok[:, r, (j3 * 4 + j) * 128:(j3 * 4 + j + 1) * 128], 128)
                                for j in range(4)], [rk])
            cp(mixT[:, j3 * 4:j3 * 4 + 4, cs], PS[bk][:, :].rearrange("p (j n) -> p j n", j=4), [PK[bk]], [f"mixT{j3}"],
               eng=("act" if j3 % 2 == 0 else "dve"))

    passB_zu(0)
    passB_sv(0)
    passB_g(0)
    for t in range(6):
        if t + 1 < 6:
            passB_zu(t + 1)
        passB_T(t)
        if t + 1 < 6:
            passB_sv(t + 1)
            passB_g(t + 1)
    ring_done(4)
    p.soft_barrier()
    PB.close()
    stage("passB")

    post_mm = []

    def wproj(blk, nk, m_lo, m_hi, rhs_fn, rhs_keys, evac, bank_base=0):
        for m in range(m_lo, m_hi):
            bi = nextbank() % 2
            for gi, (a, b_, c) in enumerate(GRPS):
                bk = bi * 2 + gi
                mm_group(PS[bk][:, 0:b_ - a], PK[bk], [(blk.w(k, m * 128, (m + 1) * 128), rhs_fn(k, a, b_)) for k in range(nk)],
                         rhs_keys + blk.keys)
                while post_mm:
                    post_mm.pop(0)()
                evac(m, gi, a, b_, c, PS[bk][:, 0:b_ - a], PK[bk])

    def resid_evac(l, which):
        def ev(mg, gi, a, b_, c, ps, pk):
            stt(xT[:, mg, a:b_], ps, mod[l][:, which * 8 + mg, c:c + 1], xT[:, mg, a:b_], OP.mult, OP.add,
                [pk, f"mod{l}_{which}", "xT"], ["xT"])
            if gi == 1:
                stats_flush(0)
                stats_chunk(mg)
        return ev

    adaln_block(0, 2)
    def wout0_steps():
        for bi_ in range(4):
            b = ring_get()
            ev = resid_evac(0, 2)
            wproj(b, 12, 0, 2, lambda k, a, b_: mixT[:, k, a:b_], ["mixT0", "mixT1", "mixT2"],
                  lambda m, gi, a, b_, c, ps, pk, bi_=bi_: ev(bi_ * 2 + m, gi, a, b_, c, ps, pk))
            ring_done()
            yield
    zip_emit(wout0_steps(), chain(adaln_gen(0, 3), adaln_gen(0, 4)))
    stats_finish()
    p.soft_barrier()
    P1b.close()
    P1.close()
    stage("wout0")

    def ffn(l):
        if l == 0:
            modulate(l, 1, chain(adaln_gen(0, 5), adaln_gen(1, 0), adaln_gen(1, 1)))
        else:
            modulate(l, 1, adaln_gen(1, 5))
        F = contextlib.ExitStack()
        aT = sb(F, "aT", [128, 32, NT], BF16)
        rtmp = sb(F, "rtmp", [128, 2, NT])
        cnt = {"i": 0}
        for bi_ in range(8):
            b = ring_get()
            def ev1(m, gi, a, b_, c, ps, pk, bi_=bi_):
                mg = bi_ * 4 + m
                r = cnt["i"] % 2
                cnt["i"] += 1
                act(rtmp[:, r, a:b_], ps, AF.Relu, [pk], [f"rtmp{r}"])
                tt(aT[:, mg, a:b_], rtmp[:, r, a:b_], ps, OP.mult, [f"rtmp{r}", pk], ["aT"])
            wproj(b, 8, 0, 4, lambda k, a, b_: hT[:, k, a:b_], ["hT0", "hT1"], ev1)
            ring_done()
        stat["tok"] = (l == 1)
        for bi_ in range(4):
            b = ring_get()
            ev = resid_evac(l, 5)
            wproj(b, 32, 0, 2, lambda k, a, b_: aT[:, k, a:b_], ["aT"],
                  lambda m, gi, a, b_, c, ps, pk, bi_=bi_: ev(bi_ * 2 + m, gi, a, b_, c, ps, pk))
            ring_done()
        stats_finish(final=(l == 1))
        p.soft_barrier()
        F.close()

    ffn(0)
    stage("ffn0")

    L1 = contextlib.ExitStack()
    modulate(1, 0, adaln_gen(1, 2))
    M1 = contextlib.ExitStack()
    bgT = sb(M1, "bgT", [128, 8, NT])
    cgT = sb(M1, "cgT", [128, 8, NT])
    xc = sb(M1, "xc", [128, 2, NT], BF16)
    mix1T = sb(M1, "mix1T", [128, 8, NT], BF16)
    dg = sb(M1, "dg", [128, 24, 128], BF16)
    for j in range(24):
        ts(dg[:, j, :], ident[:, :], vecBT[:, 56 + j:57 + j], None, OP.mult, None, ["ident", "vecBT"], ["dg"])
    SEGS = [(2, 256), (4, 64)]
    conv_pending = []
    for part in range(3):
        for bi_ in range(2):
            b = ring_get()
            def ev(m, gi, a, b_, c, ps, pk, part=part, bi_=bi_):
                mg = bi_ * 4 + m
                if part == 0:
                    cp(bgT[:, mg, a:b_], ps, [pk], ["bgT"], eng="act")
                elif part == 1:
                    cp(cgT[:, mg, a:b_], ps, [pk], ["cgT"], eng="act")
                else:
                    r = mg % 2
                    tt(xc[:, r, a:b_], cgT[:, mg, a:b_], ps, OP.mult, ["cgT", pk], [f"xc{r}{gi}"])
                    bk = 4 + 2 * r + gi
                    nseg, L = SEGS[gi]
                    n = b_ - a
                    def emit(e, mg=mg, r=r, a=a, b_=b_, n=n, L=L, bk=bk, nseg=nseg):
                        inst = e.matmul(PS[bk][:, 0:n], dg[:, 8 + mg, :], xc[:, r, a:b_], start=True, stop=False)
                        for sgi in range(nseg):
                            o0 = sgi * L
                            last = (sgi == nseg - 1)
                            e.matmul(PS[bk][:, o0 + 1:o0 + L], dg[:, mg, :], xc[:, r, a + o0:a + o0 + L - 1], start=False, stop=False)
                            inst = e.matmul(PS[bk][:, o0:o0 + L - 1], dg[:, 16 + mg, :], xc[:, r, a + o0 + 1:a + o0 + L],
                                            start=False, stop=last)
                        return inst
                    def later(emit=emit, r=r, gi=gi, bk=bk, mg=mg, a=a, b_=b_, n=n):
                        p.op("pe", emit, reads=["dg", f"xc{r}{gi}"], writes=[PK[bk]])
                        tt(mix1T[:, mg, a:b_], bgT[:, mg, a:b_], PS[bk][:, 0:n], OP.mult, ["bgT", PK[bk]], ["mix1T"])
                    post_mm.append(later)
            wproj(b, 8, 0, 4, lambda k, a, b_: hT[:, k, a:b_], ["hT0", "hT1"], ev)
            ring_done()
    while post_mm:
        post_mm.pop(0)()
    def wout1_steps():
        for bi_ in range(2):
            b = ring_get()
            ev = resid_evac(1, 2)
            wproj(b, 8, 0, 4, lambda k, a, b_: mix1T[:, k, a:b_], ["mix1T"],
                  lambda m, gi, a, b_, c, ps, pk, bi_=bi_: ev(bi_ * 4 + m, gi, a, b_, c, ps, pk))
            ring_done()
            yield
    zip_emit(wout1_steps(), chain(adaln_gen(1, 3), adaln_gen(1, 4)))
    stats_finish()
    p.soft_barrier()
    M1.close()
    stage("mix1")
    ffn(1)
    stage("ffn1")

    FN = contextlib.ExitStack()
    dfn = sb(FN, "dfn", [128, 8, 128])
    ytok = [sb(FN, f"ytok{i}", [128, D]) for i in range(3)]
    p.newsem("st_y2")
    for k in range(8):
        ts(dfn[:, k, :], ident[:, :], vecBT[:, 32 + k:33 + k], None, OP.mult, None, ["ident", "vecBT"], ["dfn"])
    for t in range(6):
        for half in range(2):
            bk = (2 * t + half) % 4
            def emit(e, t=t, half=half, bk=bk):
                inst = None
                for j in range(4):
                    k = half * 4 + j
                    inst = e.matmul(PS[bk][:, j * 128:(j + 1) * 128], xT[:, k, t * 128:(t + 1) * 128], dfn[:, k, :],
                                    start=True, stop=True)
                return inst
            p.op("pe", emit, reads=["xT", "dfn"], writes=[PK[bk]])
            if half == 0:
                act(ytok[t % 3][:, 0:512], PS[bk][:, :], AF.Copy, [PK[bk], "st6b"], [f"ytok{t % 3}a"], scale=st6[:, 8 + t:9 + t])
            else:
                ts(ytok[t % 3][:, 512:1024], PS[bk][:, :], st6[:, 8 + t:9 + t], None, OP.mult, None, [PK[bk], "st6b"], [f"ytok{t % 3}b"])
        dma("sp", y_out[t * 128:(t + 1) * 128, :], ytok[t % 3][:, :], [f"ytok{t % 3}a", f"ytok{t % 3}b"], [], f"st_y{t % 3}")
    p.final_wait("sp")
    FN.close()
    L1.close()
    L0.close()
    top.close()
    assert rstate["next_use"] == len(blocks), (rstate["next_use"], len(blocks))


_CACHE = {}


def _consts():
    j = np.arange(128, dtype=np.float32)[:, None]
    i = np.arange(128, dtype=np.float32)[None, :]
    sdk = np.float32(128 ** -0.5)
    cst = np.zeros((128, 6, 128), np.float32)
    cst[:, 0] = np.maximum(i - j, 0)
    cst[:, 1] = np.maximum(j - i, 0)
    cst[:, 2] = (i >= j).astype(np.float32) * sdk
    cst[:, 3] = (j >= i).astype(np.float32) * sdk
    cst[:, 4] = np.broadcast_to(i + 1, (128, 128))
    cst[:, 5] = np.broadcast_to(128 - i, (128, 128))
    return cst


def kernel(**inp):
    f = lambda k: np.ascontiguousarray(np.asarray(inp[k], dtype=np.float32))
    if "nc" not in _CACHE:
        _CACHE["nc"] = build_program()
    nc = _CACHE["nc"]
    x_prompt, x_sample = f("x_prompt"), f("x_sample")
    state, c, c_ctx = f("state_l0_ret"), f("c"), f("c_ctx")
    cst = _consts()
    ident = np.eye(128, dtype=np.float32)
    vecA = np.concatenate([f("l0_b_mod").reshape(48, 128), f("l1_b_mod").reshape(48, 128)], 0)
    rde = np.ascontiguousarray(np.broadcast_to(f("l0_ret_decay_exp").reshape(1, 8), (128, 8)))
    shared = {
        "vecA": vecA, "cst": cst, "rde": rde, "ident_d": ident, "gws": f("l0_gmlp_ws"),
        "l0_w_in": f("l0_w_in"), "l0_w_out": f("l0_w_out"), "l1_w_in": f("l1_w_in"), "l1_w_out": f("l1_w_out"),
    }
    for l in (0, 1):
        shared[f"l{l}_w_mod"] = f(f"l{l}_w_mod")
        shared[f"l{l}_ffn_w1"] = f(f"l{l}_ffn_w1")
        shared[f"l{l}_ffn_w2"] = f(f"l{l}_ffn_w2")
    in_maps = []
    pos = np.arange(128, dtype=np.float32)
    for core in range(8):
        b, q = core // 4, core % 4
        own = slice(256 * q, 256 * q + 256)
        xin = np.concatenate([x_prompt[2 * core], x_prompt[2 * core + 1], x_sample[b, own]], 0)
        oth_idx = np.concatenate([np.arange(0, 256 * q), np.arange(256 * q + 256, 1024)])
        xoth = x_sample[b, oth_idx]
        vecB = np.concatenate([
            f("l0_norm1").reshape(8, 128), f("l0_norm2").reshape(8, 128), f("l1_norm1").reshape(8, 128),
            f("l1_norm2").reshape(8, 128), f("final_norm").reshape(8, 128), c_ctx.reshape(8, 128), c[b].reshape(8, 128),
            f("l1_conv_w").reshape(24, 128), f("l0_gmlp_b").reshape(4, 128)], 0)
        cols = np.zeros((128, 16), np.float32)
        cols[:, 0] = 127 - pos
        cols[:, 1] = pos
        cols[:, 2] = 256 * q
        cols[:, 3] = 768 - 256 * q
        T = oth_idx.astype(np.float32).reshape(6, 128)
        tstart, tend = 256 * q, 256 * q + 255
        BIG = np.float32(1e6)
        for t in range(6):
            ef = np.where(T[t] < tstart, tstart - 1 - T[t], BIG)
            eb = np.where(T[t] > tend, T[t] - tend - 1, BIG)
            cols[:, 4 + 2 * t] = ef
            cols[:, 5 + 2 * t] = eb
        m = dict(shared)
        m.update({"xin": np.ascontiguousarray(xin), "xoth": np.ascontiguousarray(xoth), "s0": np.ascontiguousarray(state[b]),
                  "vecB": np.ascontiguousarray(vecB), "cols": cols})
        in_maps.append(m)
    res = run_bass_kernel_spmd(nc, in_maps, core_ids=list(range(8)))
    y_prompt = np.zeros((16, 256, D), np.float32)
    y_sample = np.zeros((2, 1024, D), np.float32)
    new_state = np.zeros((16, 2, 4, 128, 256), np.float32)
    for core in range(8):
        r = res.results[core]
        b, q = core // 4, core % 4
        y = np.asarray(r["y"])
        y_prompt[2 * core] = y[0:256]
        y_prompt[2 * core + 1] = y[256:512]
        y_sample[b, 256 * q:256 * q + 256] = y[512:768]
        new_state[2 * core:2 * core + 2] = np.asarray(r["ns"])
    return (y_prompt, y_sample, new_state)
```
